# Optimizing a Trainium2 kernel written in Bass

```python
import math
import jax, jax.numpy as jnp
from jax import lax
import numpy as np

D_MODEL = 1024
BATCH = 2
SEQ = 16384
DEPTH = 1
DEC_BATCH = 8
DEC_SEQ = 4096
PAST_LEN = 128

MIX_WIDTH = D_MODEL
CONV_CH = MIX_WIDTH // 2
CONV_GROUPS = 8
CONV_K = 3
N_HEADS = 8
V_HEAD = (MIX_WIDTH - CONV_CH) // N_HEADS
QK_NOPE = 64
QK_ROPE = 32
Q_LORA = D_MODEL // 4
KV_LORA = D_MODEL // 8
ROPE_BASE = 10000.0
ATTN_BLOCK = 128
IN_COLS = 3 * CONV_CH + Q_LORA + KV_LORA + QK_ROPE
PEER_HEADS = 8
PEER_NKEYS = 128
PEER_N = PEER_NKEYS * PEER_NKEYS
PEER_DKEY = 256
PEER_TOPK = 16
PEER_CHUNK = 128
ALPHA = (2.0 * DEPTH) ** 0.25
BETA = (8.0 * DEPTH) ** -0.25
LN_EPS = 1e-5
RMS_EPS = 1e-6

kernel_name = "hybrid_conv_mla_peer_encoder"


def _layernorm(x, g, b):
    xf = x.astype(jnp.float32)
    mu = jnp.mean(xf, axis=-1, keepdims=True)
    var = jnp.mean(jnp.square(xf - mu), axis=-1, keepdims=True)
    return ((xf - mu) * lax.rsqrt(var + LN_EPS) * g.astype(jnp.float32) + b.astype(jnp.float32)).astype(x.dtype)


def _rmsnorm(x, g):
    xf = x.astype(jnp.float32)
    ms = jnp.mean(jnp.square(xf), axis=-1, keepdims=True)
    return (xf * lax.rsqrt(ms + RMS_EPS) * g.astype(jnp.float32)).astype(x.dtype)


def _rope_tables(seq_len):
    inv = 1.0 / (ROPE_BASE ** (jnp.arange(0, QK_ROPE, 2, dtype=jnp.float32) / QK_ROPE))
    ang = jnp.arange(seq_len, dtype=jnp.float32)[:, None] * inv[None, :]
    return jnp.cos(ang), jnp.sin(ang)


def _apply_rope(x, cos, sin):
    x1, x2 = jnp.split(x.astype(jnp.float32), 2, axis=-1)
    return jnp.concatenate([x1 * cos - x2 * sin, x1 * sin + x2 * cos], axis=-1).astype(x.dtype)


def _short_conv(h, b_gate, c_gate, conv_w):
    u = c_gate * h
    y = lax.conv_general_dilated(
        u, conv_w[:, None, :].astype(u.dtype), window_strides=(1,),
        padding=[(CONV_K // 2, CONV_K // 2)],
        dimension_numbers=("NWC", "WIO", "NWC"),
        feature_group_count=CONV_CH)
    return b_gate * y


def _mla(q_lat, kv_lat, k_rope, q_norm_g, w_uq, kv_norm_g, w_ukv):
    bn, s_len, _ = q_lat.shape
    q = (_rmsnorm(q_lat, q_norm_g) @ w_uq).reshape(bn, s_len, N_HEADS, QK_NOPE + QK_ROPE)
    q_nope, q_rope = q[..., :QK_NOPE], q[..., QK_NOPE:]
    kv = (_rmsnorm(kv_lat, kv_norm_g) @ w_ukv).reshape(bn, s_len, N_HEADS, QK_NOPE + V_HEAD)
    k_nope, v = kv[..., :QK_NOPE], kv[..., QK_NOPE:]
    cos, sin = _rope_tables(s_len)
    q_rope = _apply_rope(q_rope, cos[:, None, :], sin[:, None, :])
    k_rope = _apply_rope(k_rope, cos, sin)
    scale = (QK_NOPE + QK_ROPE) ** -0.5
    n_blocks = s_len // ATTN_BLOCK

    def block(i):
        start = i * ATTN_BLOCK
        qn = lax.dynamic_slice_in_dim(q_nope, start, ATTN_BLOCK, axis=1)
        qr = lax.dynamic_slice_in_dim(q_rope, start, ATTN_BLOCK, axis=1)
        s = (jnp.einsum("bqhd,bkhd->bhqk", qn, k_nope, preferred_element_type=jnp.float32)
             + jnp.einsum("bqhr,bkr->bhqk", qr, k_rope, preferred_element_type=jnp.float32))
        p = jax.nn.softmax(s * scale, axis=-1).astype(v.dtype)
        return jnp.einsum("bhqk,bkhd->bqhd", p, v)

    o = lax.map(block, jnp.arange(n_blocks))
    return o.transpose(1, 0, 2, 3, 4).reshape(bn, s_len, N_HEADS * V_HEAD)


def _peer(x, wq, k1, k2, u, v):
    bn, s_len, d = x.shape
    t = x.reshape(-1, d)
    n_tok = t.shape[0]
    half = PEER_DKEY // 2

    def chunk_fn(tc):
        c = tc.shape[0]
        q = (tc @ wq).reshape(c, PEER_HEADS, 2, half)
        s1 = jnp.einsum("thd,hnd->thn", q[:, :, 0], k1, preferred_element_type=jnp.float32)
        s2 = jnp.einsum("thd,hnd->thn", q[:, :, 1], k2, preferred_element_type=jnp.float32)
        v1, i1 = lax.top_k(s1, PEER_TOPK)
        v2, i2 = lax.top_k(s2, PEER_TOPK)
        cand = (v1[..., :, None] + v2[..., None, :]).reshape(c, PEER_HEADS, PEER_TOPK * PEER_TOPK)
        cidx = (i1[..., :, None] * PEER_NKEYS + i2[..., None, :]).reshape(c, PEER_HEADS, PEER_TOPK * PEER_TOPK)
        best, pos = lax.top_k(cand, PEER_TOPK)
        eidx = jnp.take_along_axis(cidx, pos, axis=-1)
        g = jax.nn.softmax(best, axis=-1)
        a = jax.nn.gelu(jnp.einsum("td,thkd->thk", tc, u[eidx]).astype(jnp.float32), approximate=False)
        coef = (g * a).astype(v.dtype)
        return jnp.einsum("thk,thkd->td", coef, v[eidx])

    out = lax.map(chunk_fn, t.reshape(n_tok // PEER_CHUNK, PEER_CHUNK, d))
    return out.reshape(bn, s_len, d)


def _layer(x, w_in, conv_w, q_norm_g, w_uq, kv_norm_g, w_ukv, w_o, ln1_g, ln1_b,
           peer_wq, peer_k1, peer_k2, peer_u, peer_v, ln2_g, ln2_b):
    z = x @ w_in
    o1 = CONV_CH
    o2 = 2 * CONV_CH
    o3 = 3 * CONV_CH
    o4 = o3 + Q_LORA
    o5 = o4 + KV_LORA
    b_gate, c_gate, h, q_lat, kv_lat, k_rope = jnp.split(z, [o1, o2, o3, o4, o5], axis=-1)
    conv_out = _short_conv(h, b_gate, c_gate, conv_w)
    attn_out = _mla(q_lat, kv_lat, k_rope, q_norm_g, w_uq, kv_norm_g, w_ukv)
    mix = jnp.concatenate([conv_out, attn_out], axis=-1) @ w_o
    x = _layernorm(ALPHA * x + mix, ln1_g, ln1_b)
    x = _layernorm(ALPHA * x + _peer(x, peer_wq, peer_k1, peer_k2, peer_u, peer_v), ln2_g, ln2_b)
    return x


def setup_inputs(seed: int = 0) -> dict:
    key = jax.random.key(seed)
    ks = jax.random.split(key, 24)
    f32 = jnp.float32
    nrm = lambda k, shape, s: jax.random.normal(k, shape, f32) * s
    L = DEPTH
    return {
        "x_prompt": jax.random.normal(ks[0], (BATCH, SEQ, D_MODEL), f32),
        "x_sample": jax.random.normal(ks[1], (DEC_BATCH, DEC_SEQ, D_MODEL), f32),
        "w_in": nrm(ks[2], (L, D_MODEL, IN_COLS), D_MODEL ** -0.5),
        "conv_w": nrm(ks[3], (L, CONV_K, CONV_CH), CONV_K ** -0.5),
        "q_norm_g": 1.0 + nrm(ks[4], (L, Q_LORA), 0.02),
        "w_uq": nrm(ks[5], (L, Q_LORA, N_HEADS * (QK_NOPE + QK_ROPE)), Q_LORA ** -0.5),
        "kv_norm_g": 1.0 + nrm(ks[6], (L, KV_LORA), 0.02),
        "w_ukv": nrm(ks[7], (L, KV_LORA, N_HEADS * (QK_NOPE + V_HEAD)), KV_LORA ** -0.5),
        "w_o": nrm(ks[8], (L, MIX_WIDTH, D_MODEL), BETA * MIX_WIDTH ** -0.5),
        "ln1_g": 1.0 + nrm(ks[9], (L, D_MODEL), 0.02),
        "ln1_b": nrm(ks[10], (L, D_MODEL), 0.02),
        "peer_wq": nrm(ks[11], (L, D_MODEL, PEER_HEADS * PEER_DKEY), D_MODEL ** -0.5),
        "peer_k1": nrm(ks[12], (L, PEER_HEADS, PEER_NKEYS, PEER_DKEY // 2), (PEER_DKEY // 2) ** -0.5),
        "peer_k2": nrm(ks[13], (L, PEER_HEADS, PEER_NKEYS, PEER_DKEY // 2), (PEER_DKEY // 2) ** -0.5),
        "peer_u": nrm(ks[14], (L, PEER_N, D_MODEL), D_MODEL ** -0.5),
        "peer_v": nrm(ks[15], (L, PEER_N, D_MODEL), BETA * PEER_HEADS ** -0.5),
        "ln2_g": 1.0 + nrm(ks[16], (L, D_MODEL), 0.02),
        "ln2_b": nrm(ks[17], (L, D_MODEL), 0.02),
    }


def reference(x_prompt, x_sample, w_in, conv_w, q_norm_g, w_uq, kv_norm_g, w_ukv, w_o,
              ln1_g, ln1_b, peer_wq, peer_k1, peer_k2, peer_u, peer_v, ln2_g, ln2_b):
    y_prompt = x_prompt
    y_sample = x_sample
    for l in range(DEPTH):
        y_prompt = _layer(y_prompt, w_in[l], conv_w[l], q_norm_g[l], w_uq[l], kv_norm_g[l], w_ukv[l],
                          w_o[l], ln1_g[l], ln1_b[l], peer_wq[l], peer_k1[l], peer_k2[l],
                          peer_u[l], peer_v[l], ln2_g[l], ln2_b[l])
        y_sample = _layer(y_sample, w_in[l], conv_w[l], q_norm_g[l], w_uq[l], kv_norm_g[l], w_ukv[l],
                          w_o[l], ln1_g[l], ln1_b[l], peer_wq[l], peer_k1[l], peer_k2[l],
                          peer_u[l], peer_v[l], ln2_g[l], ln2_b[l])
    return (y_prompt, y_sample)
```

```python
import contextlib
import numpy as np
import concourse.bass as bass
import concourse.mybir as mybir
from concourse.bass_utils import run_bass_kernel_spmd

F32 = mybir.dt.float32
BF16 = mybir.dt.bfloat16
AF = mybir.ActivationFunctionType
ALU = mybir.AluOpType
AX = mybir.AxisListType

ALPHA = 2.0 ** 0.25
LN_EPS = 1e-5
RMS_EPS = 1e-6
NCORES = 8
NQ = 8192
NKV = 20480
DEBUG = set()
import os as _os
SKIP6 = set(_os.environ.get('SKIP6', ''))


def mkap(base, dims, off=0):
    return bass.AP(tensor=base.tensor, offset=base.offset + off,
                   ap=[list(base.ap[0])] + [list(d) for d in dims])


class Prog:
    uid = 0

    def __init__(self, nc):
        self.nc = nc
        self.ops = []
        self.last_w = {}
        self.readers = {}

    def op(self, eng, fn, r=(), w=(), sem=None):
        idx = len(self.ops)
        deps = set()
        for k in r:
            if k in self.last_w:
                deps.add(self.last_w[k])
        for k in w:
            if k in self.last_w:
                deps.add(self.last_w[k])
            for rd in self.readers.get(k, ()):
                deps.add(rd)
        o = dict(eng=eng, fn=fn, deps=deps, dma=sem is not None, sem=sem, signaled=False, tok=None)
        self.ops.append(o)
        for k in w:
            self.last_w[k] = idx
            self.readers[k] = []
        for k in r:
            self.readers.setdefault(k, []).append(idx)
        return idx

    def dma(self, out, in_, r=(), w=(), sem=None, q="sp"):
        return self.op(q, lambda e: e.dma_start(out=out, in_=in_), r=r, w=w, sem=sem)

    def emit(self, final_waits=()):
        nc = self.nc
        ops = self.ops
        for o in ops:
            nd = set()
            for d in o["deps"]:
                p = ops[d]
                if p["eng"] == "pe" and o["eng"] == "pe":
                    continue
                nd.add(d)
                p["signaled"] = True
            o["deps"] = nd
        es = contextlib.ExitStack()
        sems = {}
        cnt = {}
        for o in ops:
            if o["dma"]:
                key = "d_" + o["sem"]
                cnt[key] = cnt.get(key, 0) + 16
                o["tok"] = (key, cnt[key])
            elif o["signaled"]:
                key = "c_" + o["eng"]
                cnt[key] = cnt.get(key, 0) + 1
                o["tok"] = (key, cnt[key])
        Prog.uid += 1
        for k in cnt:
            sems[k] = es.enter_context(nc.semaphore(f"sem{Prog.uid}_{k}"))
        per_eng = {}
        for i, o in enumerate(ops):
            per_eng.setdefault(o["eng"], []).append(i)
        fin = {}
        for d in final_waits:
            s, v = ops[d]["tok"]
            fin[s] = max(fin.get(s, 0), v)
        block = es.enter_context(nc.Block())
        engmap = {"pe": "tensor", "act": "scalar", "dve": "vector", "pool": "gpsimd", "sp": "sync"}

        def make_body(idxs):
            def body(e):
                waited = {}
                for i in idxs:
                    o = ops[i]
                    need = {}
                    for d in o["deps"]:
                        s, v = ops[d]["tok"]
                        if v > need.get(s, 0):
                            need[s] = v
                    for s, v in need.items():
                        if waited.get(s, 0) < v:
                            e.wait_ge(sems[s], v)
                            waited[s] = v
                    ins = o["fn"](e)
                    if o["tok"] is not None:
                        ins.then_inc(sems[o["tok"][0]], 16 if o["dma"] else 1)
                for s, v in fin.items():
                    e.wait_ge(sems[s], v)
            return body
        for engname in ("sp", "pe", "act", "dve", "pool"):
            getattr(block, engmap[engname])(make_body(per_eng.get(engname, [])))
        es.close()


class Ctx:
    pass


def declare(nc):
    c = Ctx()

    def din(name, shape, dt=F32):
        return nc.dram_tensor(name, list(shape), dt, kind="ExternalInput").ap()

    def dsc(name, shape, dt=BF16):
        return nc.dram_tensor(name, list(shape), dt, kind="Internal").ap()
    c.xTq = din("xTq", [1024, NQ])
    c.xTh = din("xTh", [1024, 32])
    c.xTkv = din("xTkv", [1024, 16384])
    c.xtok = din("xtok", [NQ, 1024])
    c.rope_kv = din("rope_kv", [16384, 64])
    c.rope_q = din("rope_q", [NQ, 64])
    c.ident = din("ident", [128, 128])
    c.w_in = din("w_in", [1024, 1952])
    c.conv_wT = din("conv_wT", [512, 3])
    c.q_norm_g = din("q_norm_g", [1, 256])
    c.w_uq = din("w_uq", [256, 768])
    c.kv_norm_g = din("kv_norm_g", [1, 128])
    c.w_ukv = din("w_ukv", [128, 1024])
    c.w_o = din("w_o", [1024, 1024])
    c.ln1_g = din("ln1_g", [1, 1024])
    c.ln1_b = din("ln1_b", [1, 1024])
    c.peer_wq = din("peer_wq", [1024, 2048])
    c.k1T = din("k1T", [128, 8, 128])
    c.k2T = din("k2T", [128, 8, 128])
    c.uT = din("uT", [1024, 16384])
    c.v = din("v", [16384, 1024])
    c.ln2_g = din("ln2_g", [1, 1024])
    c.ln2_b = din("ln2_b", [1, 1024])
    c.y = nc.dram_tensor("y", [NQ, 1024], F32, kind="ExternalOutput").ap()
    c.uT_bf = dsc("uT_bf", [1024, 16384])
    c.v_bf = dsc("v_bf", [16384, 1024])
    c.wq_bf = dsc("wq_bf", [8, 128, 8, 256])
    c.wo_bf = dsc("wo_bf", [4, 128, 8, 256])
    c.x1_s = dsc("x1_s", [NQ, 1024], F32)
    c.KT_s = dsc("KT_s", [8, 96, NKV])
    c.V_s = dsc("V_s", [8, 128, NKV // 128, 128])
    c.QT_s = dsc("QT_s", [8, 96, NQ])
    c.mixT_s = dsc("mixT_s", [1024, NQ])
    return c


class Phase:
    def __init__(self, nc):
        self.nc = nc
        self.es = contextlib.ExitStack()
        self.P = Prog(nc)
        self.stores = []

    def sb(self, name, shape, dt=F32):
        return self.es.enter_context(self.nc.sbuf_tensor(name, list(shape), dt))

    def ps(self, name, shape, dt=F32):
        return self.es.enter_context(self.nc.psum_tensor(name, list(shape), dt))

    def store(self, out, in_, r, sem):
        i = self.P.dma(out, in_, r=r, sem=sem)
        self.stores.append(i)
        return i

    def finish(self):
        self.P.emit(final_waits=self.stores)
        self.es.close()


CAST_ENG = ("dve", "act", "pool")


def cast(P, eng, out, in_, r, w):
    if eng == "act":
        return P.op("act", lambda e: e.activation(out=out, in_=in_, func=AF.Copy), r=r, w=w)
    return P.op(eng, lambda e: e.tensor_copy(out=out, in_=in_), r=r, w=w)


def load_weight_bf16(ph, name, dram_view, shape, stage, stage_key, eng="dve"):
    P = ph.P
    wt = ph.sb(name, shape, BF16)
    sview = stage
    P.dma(sview, dram_view, w=[stage_key], sem="ld_" + stage_key)
    cast(P, eng, wt[:], sview, r=[stage_key], w=[name])
    return wt


def w_jobs(ph, c, q="sp", engs=CAST_ENG):
    P = ph.P
    st = [ph.sb(f"w_st{i}", [128, 2048], F32) for i in range(2)]
    ob = [ph.sb(f"w_ob{i}", [128, 2048], BF16) for i in range(2)]
    uT_v = c.uT.rearrange("(a p) n -> p a n", p=128)
    uTb_v = c.uT_bf.rearrange("(a p) n -> p a n", p=128)
    v_v = c.v.rearrange("(a p) d -> p a d", p=128)
    vb_v = c.v_bf.rearrange("(a p) d -> p a d", p=128)
    jobs = []
    for a in range(8):
        for cb in range(8):
            jobs.append((uT_v[:, a, cb * 2048:(cb + 1) * 2048], uTb_v[:, a, cb * 2048:(cb + 1) * 2048], None))
    for a2 in range(64):
        jobs.append((v_v[:, 2 * a2:2 * a2 + 2, :], vb_v[:, 2 * a2:2 * a2 + 2, :], 2))
    wq_v = c.peer_wq.rearrange("(a p) n -> p a n", p=128)
    wqb_v = c.wq_bf.rearrange("q p k n -> p k q n")
    for a in range(8):
        jobs.append((wq_v[:, a, :], wqb_v[:, a, :, :], "q8"))
    wo_v = c.w_o.rearrange("(a p) n -> p a n", p=128)
    wob_v = c.wo_bf.rearrange("q p k n -> p k q n")
    for a in range(8):
        jobs.append((wo_v[:, a, :], wob_v[:, a, :, :], "q4"))
    out = []
    for it, (src, dst, three) in enumerate(jobs):
        def rec(it=it, src=src, dst=dst, three=three):
            s = it % 2
            if three is None or three == "q8":
                sv, cv = st[s][:], ob[s][:]
                ov = ob[s][:] if three is None else ob[s][:].rearrange("p (q n) -> p q n", q=8)
            elif three == "q4":
                sv, cv = st[s][:, 0:1024], ob[s][:, 0:1024]
                ov = ob[s][:, 0:1024].rearrange("p (q n) -> p q n", q=4)
            else:
                sv = st[s][:].rearrange("p (a d) -> p a d", a=2)
                cv = ob[s][:].rearrange("p (a d) -> p a d", a=2)
                ov = cv
            P.dma(sv, src, w=[f"wst{s}"], sem=f"wst{s}", q=q)
            cast(P, engs[it % len(engs)], cv, sv, r=[f"wst{s}"], w=[f"wob{s}"])
            ph.stores.append(P.dma(dst, ov, r=[f"wob{s}"], sem=f"wob{s}", q=q))
        out.append(rec)
    return out


def phase_W(nc, c):
    ph = Phase(nc)
    for rec in w_jobs(ph, c):
        rec()
    ph.finish()


def rmsnorm_rows(ph, pfx, zsrc, width, gam, out_bf, rkeys, wkey, slot):
    P = ph.P
    ss, sd, rs, junk = ph.t[pfx + "ss"][slot], ph.t[pfx + "sd"][slot], ph.t[pfx + "rs"][slot], ph.t[pfx + "junk"][slot]
    k = f"{pfx}{slot}"
    P.op("pool", lambda e: e.memset(ss[:], 0.0), w=[k + "ss"])
    P.op("act", lambda e: e.activation(out=junk[:, 0:width], in_=zsrc, func=AF.Square, accum_out=ss[:, 0:1]),
         r=rkeys + [k + "ss"], w=[k + "ss", k + "junk"] + rkeys)
    P.op("act", lambda e: e.activation(out=sd[:], in_=ss[:], func=AF.Sqrt, bias=RMS_EPS, scale=1.0 / width),
         r=[k + "ss"], w=[k + "sd"])
    P.op("dve", lambda e: e.reciprocal(out=rs[:], in_=sd[:]), r=[k + "sd"], w=[k + "rs"])
    P.op("dve", lambda e: e.scalar_tensor_tensor(out=out_bf, in0=zsrc, scalar=rs[:, 0:1], in1=gam,
                                                 op0=ALU.mult, op1=ALU.mult),
         r=rkeys + [k + "rs", "gam"], w=[wkey] + rkeys)


def phase_A(nc, c, nkv=NKV // 128, nq=NQ // 128, upto=99):
    ph = Phase(nc)
    P = ph.P
    sb, ps = ph.sb, ph.ps
    ph.t = {}
    stage = sb("a_stage", [128, 8, 256], F32)
    idf = sb("a_idf", [128, 128], F32)
    idb = sb("a_idb", [128, 128], BF16)
    P.dma(idf[:], c.ident[:, :], w=["idf"], sem="idf")
    cast(P, "dve", idb[:], idf[:], r=["idf"], w=["idb"])
    w_in_v = c.w_in.rearrange("(kc p) n -> p kc n", p=128)
    wkv = load_weight_bf16(ph, "wkv", w_in_v[:, :, 1792:1952], [128, 8, 160], stage[:, :, 0:160], "stage", "dve")
    wq_ = load_weight_bf16(ph, "wq", w_in_v[:, :, 1536:1792], [128, 8, 256], stage[:, :, 0:256], "stage", "act")
    wukv = load_weight_bf16(ph, "wukv", c.w_ukv[:, :], [128, 1024], stage[:].rearrange("p a b -> p (a b)")[:, 0:1024], "stage", "dve")
    wuq = load_weight_bf16(ph, "wuq", c.w_uq.rearrange("(kc p) n -> p kc n", p=128), [128, 2, 768],
                           stage[:].rearrange("p a b -> p (a b)")[:, 0:1536].rearrange("p (k n) -> p k n", k=2), "stage", "act")
    gkv = sb("a_gkv", [128, 128], F32)
    gq = sb("a_gq", [128, 256], F32)
    P.dma(gkv[:], c.kv_norm_g[0:1, :].partition_broadcast(128), w=["gam"], sem="gam")
    P.dma(gq[:], c.q_norm_g[0:1, :].partition_broadcast(128), w=["gam"], sem="gam")

    NS = 2
    xs = [sb(f"a_xs{i}", [128, 8, 128], F32) for i in range(3)]
    xb = [sb(f"a_xb{i}", [128, 8, 128], BF16) for i in range(3)]
    rp = [sb(f"a_rp{i}", [128, 64], F32) for i in range(3)]
    for pfx, wd in (("kv", 128), ("q", 256)):
        ph.t[pfx + "ss"] = [sb(f"a_{pfx}ss{i}", [128, 1], F32) for i in range(NS)]
        ph.t[pfx + "sd"] = [sb(f"a_{pfx}sd{i}", [128, 1], F32) for i in range(NS)]
        ph.t[pfx + "rs"] = [sb(f"a_{pfx}rs{i}", [128, 1], F32) for i in range(NS)]
        ph.t[pfx + "junk"] = [sb(f"a_{pfx}jk{i}", [128, 256], F32) for i in range(NS)]
    kvn = [sb(f"a_kvn{i}", [128, 128], BF16) for i in range(NS)]
    kvnT = [sb(f"a_kvnT{i}", [128, 128], BF16) for i in range(NS)]
    tA = [sb(f"a_tA{i}", [128, 8, 32], F32) for i in range(NS)]
    tB = [sb(f"a_tB{i}", [128, 8, 32], F32) for i in range(NS)]
    kr = [sb(f"a_kr{i}", [128, 32], F32) for i in range(NS)]
    Ktok = [sb(f"a_Ktok{i}", [128, 8, 96], BF16) for i in range(NS)]
    Vtok = [sb(f"a_Vtok{i}", [128, 8, 128], BF16) for i in range(NS)]
    KTs = [sb(f"a_KTs{i}", [96, 8, 128], BF16) for i in range(NS)]
    qn = [sb(f"a_qn{i}", [128, 256], BF16) for i in range(NS)]
    qnT = [sb(f"a_qnT{i}", [128, 2, 128], BF16) for i in range(NS)]
    zp = [ps(f"a_zp{i}", [128, 512], F32) for i in range(2)]
    trp = ps("a_trp", [128, 1024], BF16)
    kvp = ps("a_kvp", [128, 1024], F32)
    KTp = [ps(f"a_KTp{i}", [128, 1024], BF16) for i in range(2)]
    for i in range(NS):
        P.op("pool", lambda e, i=i: e.memset(Vtok[i][:], 0.0), w=[f"Vtok{i}"])
        P.op("pool", lambda e, i=i: e.memset(Vtok[i][:, :, 0:1], 1.0), w=[f"Vtok{i}"])

    xTkv_v = c.xTkv.rearrange("(kc p) t -> p kc t", p=128)
    xTq_v = c.xTq.rearrange("(kc p) t -> p kc t", p=128)
    KT_v = c.KT_s.rearrange("h r t -> r h t")
    V_v = c.V_s.rearrange("h p c e -> p h c e")
    QT_v = c.QT_s.rearrange("h r t -> r h t")

    def load_x(it, src_view, t0, rope_src, r0):
        s3 = it % 3
        P.dma(xs[s3][:], src_view[:, :, t0:t0 + 128], w=[f"xs{s3}"], sem=f"xs{s3}")
        P.dma(rp[s3][:], rope_src[r0:r0 + 128, :], w=[f"rp{s3}"], sem=f"rp{s3}")
        cast(P, "act" if it % 2 == 0 else "dve", xb[s3][:], xs[s3][:], r=[f"xs{s3}"], w=[f"xb{s3}"])

    def kv_load(it):
        if it < 128:
            load_x(it, xTkv_v, it * 128, c.rope_kv, it * 128)
        else:
            load_x(it, xTq_v, 4096 + (it - 128) * 128, c.rope_kv, (it - 128) * 128)

    def kv_s1(it):
        s, s3 = it % NS, it % 3
        z = zp[s]
        for kc in range(8):
            P.op("pe", lambda e, kc=kc: e.matmul(z[:, 0:160], xb[s3][:, kc, :], wkv[:, kc, :], start=(kc == 0), stop=(kc == 7)),
                 r=[f"xb{s3}", "wkv"], w=[f"zp{s}"])
        rmsnorm_rows(ph, "kv", z[:, 0:128], 128, gkv[:], kvn[s][:], [f"zp{s}"], f"kvn{s}", s)
        P.op("dve", lambda e: e.tensor_tensor(out=tA[s][:, 0, :], in0=z[:, 128:160], in1=rp[s3][:, 0:32], op=ALU.mult),
             r=[f"zp{s}", f"rp{s3}"], w=[f"tA{s}0", f"tA{s}1", f"zp{s}"])
        P.op("dve", lambda e: e.tensor_tensor(out=tB[s][:, 0, 0:16], in0=z[:, 144:160], in1=rp[s3][:, 32:48], op=ALU.mult),
             r=[f"zp{s}", f"rp{s3}"], w=[f"tB{s}a0", f"tB{s}a1", f"zp{s}"])
        P.op("dve", lambda e: e.tensor_tensor(out=tB[s][:, 0, 16:32], in0=z[:, 128:144], in1=rp[s3][:, 48:64], op=ALU.mult),
             r=[f"zp{s}", f"rp{s3}"], w=[f"tB{s}b0", f"tB{s}b1", f"zp{s}"])
        P.op("pool", lambda e: e.tensor_tensor(out=kr[s][:], in0=tA[s][:, 0, :], in1=tB[s][:, 0, :], op=ALU.add),
             r=[f"tA{s}0", f"tB{s}a0", f"tB{s}b0"], w=[f"kr{s}"])

    def kv_s2(it):
        s = it % NS
        P.op("pe", lambda e: e.transpose(out=trp[:, 0:128], in_=kvn[s][:], identity=idb[:]), r=[f"kvn{s}", "idb"], w=["trp"])
        P.op("act", lambda e: e.activation(out=kvnT[s][:], in_=trp[:, 0:128], func=AF.Copy), r=["trp"], w=[f"kvnT{s}", "trp"])
        for half in range(2):
            P.op("pe", lambda e, half=half: e.matmul(kvp[:, half * 512:(half + 1) * 512], kvnT[s][:], wukv[:, half * 512:(half + 1) * 512], start=True, stop=True),
                 r=[f"kvnT{s}", "wukv"], w=[f"kvp{half}"])
        for half in range(2):
            kv4 = kvp[:, half * 512:(half + 1) * 512].rearrange("p (h e) -> p h e", e=128)
            P.op("act", lambda e, kv4=kv4, half=half: e.activation(out=Ktok[s][:, half * 4:(half + 1) * 4, 32:96], in_=kv4[:, :, 0:64], func=AF.Copy),
                 r=[f"kvp{half}"], w=[f"Ktok{s}n{half}", f"kvp{half}"])
            P.op("dve", lambda e, kv4=kv4, half=half: e.tensor_copy(out=Vtok[s][:, half * 4:(half + 1) * 4, 64:128], in_=kv4[:, :, 64:128]),
                 r=[f"kvp{half}"], w=[f"Vtok{s}", f"kvp{half}"])
        P.op("pool", lambda e: e.tensor_copy(out=Ktok[s][:, :, 0:32], in_=mkap(kr[s][:], [[0, 8], [1, 32]])), r=[f"kr{s}"], w=[f"Ktok{s}r"])
        ktp = KTp[s]
        for h in range(8):
            P.op("pe", lambda e, h=h: e.transpose(out=ktp[0:96, h * 128:(h + 1) * 128], in_=Ktok[s][:, h, :], identity=idb[:]),
                 r=[f"Ktok{s}n0", f"Ktok{s}n1", f"Ktok{s}r", "idb"], w=[f"KTp{s}"])
        if it % 2 == 0:
            P.op("dve", lambda e: e.tensor_copy(out=KTs[s][:], in_=ktp[0:96, :].rearrange("p (h t) -> p h t", h=8)), r=[f"KTp{s}"], w=[f"KTs{s}", f"KTp{s}"])
        else:
            P.op("act", lambda e: e.activation(out=KTs[s][:], in_=ktp[0:96, :].rearrange("p (h t) -> p h t", h=8), func=AF.Copy), r=[f"KTp{s}"], w=[f"KTs{s}", f"KTp{s}"])
        ph.store(KT_v[:, :, it * 128:(it + 1) * 128], KTs[s][:], r=[f"KTs{s}"], sem=f"KTs{s}")
        ph.store(V_v[:, :, it, :], Vtok[s][:], r=[f"Vtok{s}"], sem=f"Vtok{s}")

    for it in range(min(2, nkv)):
        kv_load(it)
    if nkv > 0:
        kv_s1(0)
    for it in range(nkv):
        if it + 2 < nkv:
            kv_load(it + 2)
        if it + 1 < nkv:
            kv_s1(it + 1)
        kv_s2(it)

    qp = kvp

    def q_load(it):
        load_x(NKV // 128 + it, xTq_v, it * 128, c.rope_q, it * 128)

    def q_s1(it):
        s, s3 = it % NS, (NKV // 128 + it) % 3
        z = zp[s]
        for kc in range(8):
            P.op("pe", lambda e, kc=kc: e.matmul(z[:, 0:256], xb[s3][:, kc, :], wq_[:, kc, :], start=(kc == 0), stop=(kc == 7)),
                 r=[f"xb{s3}", "wq"], w=[f"zp{s}"])
        rmsnorm_rows(ph, "q", z[:, 0:256], 256, gq[:], qn[s][:], [f"zp{s}"], f"qn{s}", s)

    def q_s2(it):
        s, s3 = it % NS, (NKV // 128 + it) % 3
        for k2 in range(2):
            P.op("pe", lambda e, k2=k2: e.transpose(out=trp[:, k2 * 128:(k2 + 1) * 128], in_=qn[s][:, k2 * 128:(k2 + 1) * 128], identity=idb[:]),
                 r=[f"qn{s}", "idb"], w=["trp"])
        P.op("act", lambda e: e.activation(out=qnT[s][:], in_=trp[:, 0:256].rearrange("p (k t) -> p k t", k=2), func=AF.Copy),
             r=["trp"], w=[f"qnT{s}", "trp"])
        for (c0, c1, o0, key) in ((0, 480, 0, "kvp0"), (480, 768, 512, "kvp1")):
            for k2 in range(2):
                P.op("pe", lambda e, k2=k2, c0=c0, c1=c1, o0=o0: e.matmul(qp[:, o0:o0 + (c1 - c0)], qnT[s][:, k2, :], wuq[:, k2, c0:c1],
                                                                       start=(k2 == 0), stop=(k2 == 1)),
                     r=[f"qnT{s}", "wuq"], w=[key])
        for (h0, nh, o0, key, half) in ((0, 5, 0, "kvp0", 0), (5, 3, 512, "kvp1", 1)):
            q3 = qp[:, o0:o0 + nh * 96].rearrange("p (h e) -> p h e", e=96)
            P.op("act", lambda e, q3=q3, h0=h0, nh=nh: e.activation(out=Ktok[s][:, h0:h0 + nh, 32:96], in_=q3[:, :, 0:64], func=AF.Copy),
                 r=[key], w=[f"Ktok{s}n{half}", key])
            cc_b = mkap(rp[s3][:, 0:32], [[0, nh], [1, 32]])
            ns_b = mkap(rp[s3][:, 32:48], [[0, nh], [1, 16]])
            sn_b = mkap(rp[s3][:, 48:64], [[0, nh], [1, 16]])
            P.op("dve", lambda e, cc_b=cc_b, q3=q3, h0=h0, nh=nh: e.tensor_tensor(out=tA[s][:, h0:h0 + nh, :], in0=q3[:, :, 64:96], in1=cc_b, op=ALU.mult),
                 r=[key, f"rp{s3}"], w=[f"tA{s}{half}", key])
            P.op("dve", lambda e, ns_b=ns_b, q3=q3, h0=h0, nh=nh: e.tensor_tensor(out=tB[s][:, h0:h0 + nh, 0:16], in0=q3[:, :, 80:96], in1=ns_b, op=ALU.mult),
                 r=[key, f"rp{s3}"], w=[f"tB{s}a{half}", key])
            P.op("dve", lambda e, sn_b=sn_b, q3=q3, h0=h0, nh=nh: e.tensor_tensor(out=tB[s][:, h0:h0 + nh, 16:32], in0=q3[:, :, 64:80], in1=sn_b, op=ALU.mult),
                 r=[key, f"rp{s3}"], w=[f"tB{s}b{half}", key])
        P.op("pool", lambda e: e.tensor_tensor(out=Ktok[s][:, :, 0:32], in0=tA[s][:], in1=tB[s][:], op=ALU.add),
             r=[f"tA{s}0", f"tA{s}1", f"tB{s}a0", f"tB{s}a1", f"tB{s}b0", f"tB{s}b1"], w=[f"Ktok{s}r"])
        ktp = KTp[s]
        for h in range(8):
            P.op("pe", lambda e, h=h: e.transpose(out=ktp[0:96, h * 128:(h + 1) * 128], in_=Ktok[s][:, h, :], identity=idb[:]),
                 r=[f"Ktok{s}n0", f"Ktok{s}n1", f"Ktok{s}r", "idb"], w=[f"KTp{s}"])
        P.op("dve", lambda e: e.tensor_copy(out=KTs[s][:], in_=ktp[0:96, :].rearrange("p (h t) -> p h t", h=8)), r=[f"KTp{s}"], w=[f"KTs{s}", f"KTp{s}"])
        ph.store(QT_v[:, :, it * 128:(it + 1) * 128], KTs[s][:], r=[f"KTs{s}"], sem=f"KTs{s}")

    for it in range(min(2, nq)):
        q_load(it)
    if nq > 0:
        q_s1(0)
    for it in range(nq):
        if it + 2 < nq:
            q_load(it + 2)
        if it + 1 < nq:
            q_s1(it + 1)
        q_s2(it)
    ph.finish()


def phase_A3(nc, c):
    ph = Phase(nc)
    P = ph.P
    sb, ps = ph.sb, ph.ps
    stage = sb("c_stage", [128, 8, 1536], F32)
    w_in_v = c.w_in.rearrange("(kc p) n -> p kc n", p=128)
    wc = load_weight_bf16(ph, "wc", w_in_v[:, :, 0:1536], [128, 8, 1536], stage[:], "stage", "pool")
    cw = sb("c_cw", [128, 4, 3], F32)
    P.dma(cw[:], c.conv_wT.rearrange("(cc p) k -> p cc k", p=128), w=["cw"], sem="cw")
    xh = sb("c_xh", [128, 8, 32], F32)
    xhb = sb("c_xhb", [128, 8, 32], BF16)
    uh = sb("c_uh", [128, 4, 32], F32)
    chs = sb("c_chs", [128, 32], F32)
    P.dma(xh[:], c.xTh.rearrange("(kc p) t -> p kc t", p=128), w=["xh"], sem="xh")
    cast(P, "dve", xhb[:], xh[:], r=["xh"], w=["xhb"])
    pb = [ps(f"c_pb{i}", [128, 512], F32) for i in range(2)]
    pc = [ps(f"c_pc{i}", [128, 512], F32) for i in range(2)]
    phh = [ps(f"c_ph{i}", [128, 512], F32) for i in range(2)]
    for cc in range(4):
        for (dst, col, key) in ((pc[0], 512, "pc0"), (phh[0], 1024, "ph0")):
            for kc in range(8):
                P.op("pe", lambda e, dst=dst, col=col, kc=kc, cc=cc: e.matmul(dst[:, 0:32], wc[:, kc, col + cc * 128: col + (cc + 1) * 128],
                                                                              xhb[:, kc, :], start=(kc == 0), stop=(kc == 7)),
                     r=["wc", "xhb"], w=[key])
        P.op("act", lambda e: e.activation(out=chs[:], in_=pc[0][:, 0:32], func=AF.Copy), r=["pc0"], w=["chs"])
        P.op("dve", lambda e, cc=cc: e.tensor_tensor(out=uh[:, cc, :], in0=chs[:], in1=phh[0][:, 0:32], op=ALU.mult),
             r=["chs", "ph0"], w=["uh"])
    xs = [sb(f"c_xs{i}", [128, 8, 512], F32) for i in range(2)]
    xb = [sb(f"c_xb{i}", [128, 8, 512], BF16) for i in range(2)]
    csb = [sb(f"c_csb{i}", [128, 512], F32) for i in range(2)]
    ub = [sb(f"c_ub{i}", [128, 514], F32) for i in range(2)]
    yb = [sb(f"c_yb{i}", [128, 512], F32) for i in range(2)]
    cv = [sb(f"c_cv{i}", [128, 512], BF16) for i in range(2)]
    xTq_v = c.xTq.rearrange("(kc p) t -> p kc t", p=128)
    j = 0
    for b in range(NQ // 512):
        s = b % 2
        for hh in range(2):
            P.dma(xs[s][:, hh * 4:(hh + 1) * 4, :], xTq_v[:, hh * 4:(hh + 1) * 4, b * 512:(b + 1) * 512], w=[f"xs{s}{hh}"], sem=f"xs{s}{hh}")
            cast(P, "act" if hh == 0 else "dve", xb[s][:, hh * 4:(hh + 1) * 4, :], xs[s][:, hh * 4:(hh + 1) * 4, :],
                 r=[f"xs{s}{hh}"], w=[f"xb{s}{hh}"])
        for cc in range(4):
            t = j % 2
            j += 1
            for (dst, col, key) in ((pb[t], 0, f"pb{t}"), (pc[t], 512, f"pc{t}"), (phh[t], 1024, f"ph{t}")):
                for kc in range(8):
                    P.op("pe", lambda e, dst=dst, col=col, kc=kc, cc=cc, s=s: e.matmul(dst[:], wc[:, kc, col + cc * 128: col + (cc + 1) * 128],
                                                                                       xb[s][:, kc, :], start=(kc == 0), stop=(kc == 7)),
                         r=["wc", f"xb{s}0", f"xb{s}1"], w=[key])
            P.op("act", lambda e, t=t: e.activation(out=csb[t][:], in_=pc[t][:], func=AF.Copy), r=[f"pc{t}"], w=[f"csb{t}"])
            P.op("dve", lambda e, t=t: e.tensor_tensor(out=ub[t][:, 1:513], in0=csb[t][:], in1=phh[t][:], op=ALU.mult),
                 r=[f"csb{t}", f"ph{t}"], w=[f"ub{t}m"])
            P.op("pool", lambda e, t=t, cc=cc, b=b: e.tensor_copy(out=ub[t][:, 0:1], in_=uh[:, cc, 2 * b:2 * b + 1]), r=["uh"], w=[f"ub{t}l"])
            P.op("pool", lambda e, t=t, cc=cc, b=b: e.tensor_copy(out=ub[t][:, 513:514], in_=uh[:, cc, 2 * b + 1:2 * b + 2]), r=["uh"], w=[f"ub{t}r"])
            ubk = [f"ub{t}m", f"ub{t}l", f"ub{t}r"]
            P.op("dve", lambda e, t=t, cc=cc: e.tensor_scalar(out=yb[t][:], in0=ub[t][:, 0:512], scalar1=cw[:, cc, 0:1], scalar2=None, op0=ALU.mult),
                 r=ubk + ["cw"], w=[f"yb{t}"])
            P.op("dve", lambda e, t=t, cc=cc: e.scalar_tensor_tensor(out=yb[t][:], in0=ub[t][:, 1:513], scalar=cw[:, cc, 1:2], in1=yb[t][:],
                                                                      op0=ALU.mult, op1=ALU.add),
                 r=ubk + ["cw", f"yb{t}"], w=[f"yb{t}"])
            P.op("dve", lambda e, t=t, cc=cc: e.scalar_tensor_tensor(out=yb[t][:], in0=ub[t][:, 2:514], scalar=cw[:, cc, 2:3], in1=yb[t][:],
                                                                      op0=ALU.mult, op1=ALU.add),
                 r=ubk + ["cw", f"yb{t}"], w=[f"yb{t}"])
            P.op("dve", lambda e, t=t: e.tensor_tensor(out=cv[t][:], in0=yb[t][:], in1=pb[t][:], op=ALU.mult),
                 r=[f"yb{t}", f"pb{t}"], w=[f"cv{t}"])
            ph.store(c.mixT_s[cc * 128:(cc + 1) * 128, b * 512:(b + 1) * 512], cv[t][:], r=[f"cv{t}"], sem=f"cv{t}")
    ph.finish()


def phase_B(nc, c, with_w=True):
    ph = Phase(nc)
    P = ph.P
    sb, ps = ph.sb, ph.ps
    scale = 96.0 ** -0.5
    KT = [sb(f"b_KT{i}", [96, 16384], BF16) for i in range(2)]
    Vh = [sb(f"b_V{i}", [128, 128, 128], BF16) for i in range(2)]
    QT = [sb(f"b_QT{i}", [96, 512], BF16) for i in range(2)]
    pt = [sb(f"b_pt{i}", [128, 512], BF16) for i in range(4)]
    rd = sb("b_rd", [1, 512], F32)
    ones = sb("b_ones", [1, 128], F32)
    bcs = sb("b_bcs", [128, 512], F32)
    at = [sb(f"b_at{i}", [128, 512], BF16) for i in range(2)]
    sp_ = [ps(f"b_sp{i}", [128, 512], F32) for i in range(4)]
    op_ = [ps(f"b_op{i}", [128, 512], F32) for i in range(2)]
    bcp = ps("b_bcp", [128, 512], F32)
    P.op("pool", lambda e: e.memset(ones[:], 1.0), w=["ones"])
    heads = [(job, h) for job in range(2) for h in range(8)]

    def load_kv(hi):
        job, h = heads[hi]
        s = hi % 2
        nk = 16384 if job == 0 else 4096
        koff = 0 if job == 0 else 16384
        nq4 = nk // 4
        for qq in range(4):
            P.dma(KT[s][:, qq * nq4:(qq + 1) * nq4], c.KT_s[h, :, koff + qq * nq4: koff + (qq + 1) * nq4], w=[f"KT{s}"], sem=f"KT{s}")
        nkc = nk // 128
        P.dma(Vh[s][:, 0:nkc, :], c.V_s[h, :, koff // 128: koff // 128 + nkc, :], w=[f"V{s}"], sem=f"V{s}")

    NSP = 4
    LOOK = 3
    load_kv(0)
    jobs_q = [(hi, qb) for hi in range(16) for qb in range(8)]

    def load_q(n):
        hi, qb = jobs_q[n]
        job, h = heads[hi]
        qs = n % 2
        q0 = job * 4096 + qb * 512
        P.dma(QT[qs][:], c.QT_s[h, :, q0:q0 + 512], w=[f"QT{qs}"], sem=f"QT{qs}")
    load_q(0)
    units = []
    for n, (hi, qb) in enumerate(jobs_q):
        job, h = heads[hi]
        nkc = 128 if job == 0 else 32
        for kc in range(nkc):
            units.append((n, kc, nkc))
    loaded_q = {0}
    loaded_kv = {0}

    def issue_S(u):
        n, kc, nkc = units[u]
        hi, qb = jobs_q[n]
        if kc == 0 and n + 1 < len(jobs_q) and (n + 1) not in loaded_q:
            load_q(n + 1)
            loaded_q.add(n + 1)
        if kc == 8 and qb == 0 and hi + 1 < 16 and (hi + 1) not in loaded_kv:
            load_kv(hi + 1)
            loaded_kv.add(hi + 1)
        s, qs, k3 = hi % 2, n % 2, u % NSP
        P.op("pe", lambda e: e.matmul(sp_[k3][:], KT[s][:, kc * 128:(kc + 1) * 128], QT[qs][:], start=True, stop=True),
             r=[f"KT{s}", f"QT{qs}"], w=[f"sp{k3}"])
        P.op("act", lambda e: e.activation(out=pt[k3][:], in_=sp_[k3][:], func=AF.Exp, scale=scale),
             r=[f"sp{k3}"], w=[f"pt{k3}", f"sp{k3}"])

    def finalize(n):
        hi, qb = jobs_q[n]
        job, h = heads[hi]
        o = op_[n % 2]
        a = at[n % 2]
        P.op("dve", lambda e: e.reciprocal(out=rd[:], in_=o[0:1, :]), r=[f"op{n % 2}"], w=["rd", f"op{n % 2}"])
        P.op("pe", lambda e: e.matmul(bcp[:, :], ones[:], rd[:], start=True, stop=True), r=["ones", "rd"], w=["bcp"])
        P.op("act", lambda e: e.activation(out=bcs[64:128, :], in_=bcp[64:128, :], func=AF.Copy), r=["bcp"], w=["bcs", "bcp"])
        P.op("dve", lambda e: e.tensor_tensor(out=a[64:128, :], in0=o[64:128, :], in1=bcs[64:128, :], op=ALU.mult),
             r=[f"op{n % 2}", "bcs"], w=[f"at{n % 2}", f"op{n % 2}"])
        q0 = job * 4096 + qb * 512
        ph.store(c.mixT_s[512 + 64 * h: 512 + 64 * (h + 1), q0:q0 + 512], a[64:128, :], r=[f"at{n % 2}"], sem=f"at{n % 2}")

    pending = []
    wj = w_jobs(ph, c, q="pool", engs=("pool", "dve")) if with_w else []
    for u in range(min(LOOK, len(units))):
        issue_S(u)
    for u in range(len(units)):
        if u % 64 == 32 and wj:
            wj.pop(0)()
        if u + LOOK < len(units):
            issue_S(u + LOOK)
        n, kc, nkc = units[u]
        hi, qb = jobs_q[n]
        s, k3 = hi % 2, u % NSP
        o = op_[n % 2]
        P.op("pe", lambda e, s=s, kc=kc, k3=k3, o=o, nkc=nkc: e.matmul(o[:, :], Vh[s][:, kc, :], pt[k3][:], start=(kc == 0), stop=(kc == nkc - 1)),
             r=[f"V{s}", f"pt{k3}"], w=[f"op{n % 2}"])
        if kc == nkc - 1:
            pending.append((u + 6, n))
        while pending and (pending[0][0] <= u or u == len(units) - 1):
            finalize(pending.pop(0)[1])
    while wj:
        wj.pop(0)()
    ph.finish()


def layernorm_rows(ph, pfx, src, g_rep, b_rep, out, rkeys, wkey):
    P = ph.P
    st, mv, sd, rs = ph.t["st"], ph.t["mv"], ph.t["sd"], ph.t["rs"]
    P.op("dve", lambda e: e.bn_stats(out=st[:, 0, :], in_=src[:, 0:512]), r=rkeys, w=["st0"])
    P.op("dve", lambda e: e.bn_stats(out=st[:, 1, :], in_=src[:, 512:1024]), r=rkeys, w=["st1"])
    P.op("dve", lambda e: e.bn_aggr(out=mv[:], in_=st[:].rearrange("p a b -> p (a b)")), r=["st0", "st1"], w=["mv"])
    P.op("act", lambda e: e.activation(out=sd[:], in_=mv[:, 1:2], func=AF.Sqrt, bias=LN_EPS, scale=1.0), r=["mv"], w=["sd"])
    P.op("dve", lambda e: e.reciprocal(out=rs[:], in_=sd[:]), r=["sd"], w=["rs"])
    P.op("dve", lambda e: e.tensor_scalar(out=out, in0=src, scalar1=mv[:, 0:1], scalar2=rs[:, 0:1], op0=ALU.subtract, op1=ALU.mult),
         r=rkeys + ["mv", "rs"], w=[wkey])
    P.op("pool", lambda e: e.tensor_tensor(out=out, in0=out, in1=g_rep, op=ALU.mult), r=[wkey, "lnp"], w=[wkey])
    P.op("pool", lambda e: e.tensor_tensor(out=out, in0=out, in1=b_rep, op=ALU.add), r=[wkey, "lnp"], w=[wkey])


def phase_D(nc, c, ngroups=None):
    TG = 2
    NB = 256
    IB = 8
    ph = Phase(nc)
    P = ph.P
    sb, ps = ph.sb, ph.ps
    ph.t = {}
    stage = sb("d_stage", [128, 1024], F32)
    idf = sb("d_idf", [128, 128], F32)
    idb = sb("d_idb", [128, 128], BF16)
    P.dma(idf[:], c.ident[:, :], w=["idf"], sem="idf")
    cast(P, "dve", idb[:], idf[:], r=["idf"], w=["idb"])
    wo = sb("d_wo", [128, 8, 1024], BF16)
    wqs = [sb(f"d_wqs{i}", [128, 8, 512], BF16) for i in range(2)]
    wo_v = c.w_o.rearrange("(kc p) n -> p kc n", p=128)
    wqb_v = c.wq_bf.rearrange("(kc p) n -> p kc n", p=128)
    for kc in range(8):
        P.dma(stage[:], wo_v[:, kc, :], w=["stage"], sem="stage")
        cast(P, CAST_ENG[kc % 3], wo[:, kc, :], stage[:], r=["stage"], w=["wo"])
    WKEYS = ["wo", "wq"]
    wqn = 0
    kT = [sb(f"d_k{j}T", [128, 8, 128], BF16) for j in range(2)]
    for j, src in enumerate((c.k1T, c.k2T)):
        P.dma(stage[:, 0:1024].rearrange("p (h n) -> p h n", h=8), src[:, :, :], w=["stage"], sem="stage")
        cast(P, "dve", kT[j][:], stage[:, 0:1024].rearrange("p (h n) -> p h n", h=8), r=["stage"], w=[f"kT{j}"])
    lnp = sb("d_lnp", [128, 4, 1024], F32)
    for j, src in enumerate((c.ln1_g, c.ln1_b, c.ln2_g, c.ln2_b)):
        P.dma(lnp[:, j, :], src[0:1, :].partition_broadcast(128), w=["lnp"], sem="lnp")
    ph.t["st"] = sb("d_st", [128, 2, 6], F32)
    ph.t["mv"] = sb("d_mv", [128, 2], F32)
    ph.t["sd"] = sb("d_sd", [128, 1], F32)
    ph.t["rs"] = sb("d_rs", [128, 1], F32)
    mx = sb("d_mx", [128, 8, 128], BF16)
    xt = sb("d_xt", [128, 1024], F32)
    x1 = [sb(f"d_x1_{t}", [128, 1024], F32) for t in range(TG)]
    x1b = sb("d_x1b", [128, 1024], BF16)
    x1T = [sb(f"d_x1T{t}", [128, 8, 128], BF16) for t in range(TG)]
    qT = sb("d_qT", [128, 16, 128], BF16)
    s12 = sb("d_s12", [128, 16, 128], F32)
    wk = sb("d_wk", [128, 256], F32)
    t16 = sb("d_t16", [128, 16, 16], F32)
    cand = sb("d_cand", [128, 8, 256], F32)
    b16 = sb("d_b16", [128, 8, 16], F32)
    dd = sb("d_dd", [128, 8, 16], F32)
    ee = sb("d_ee", [128, 8, 16], F32)
    zz = sb("d_zz", [128, 8], F32)
    lz = sb("d_lz", [128, 8], F32)
    mz = sb("d_mz", [128, 8], F32)
    th = sb("d_th", [128, 8], F32)
    a1 = sb("d_a1", [128, 8, 128], F32)
    G = [sb(f"d_G{t}", [128, 16384], BF16) for t in range(TG)]
    xx = [sb(f"d_xx{i}", [128, IB * 128], F32) for i in range(2)]
    ex = [sb(f"d_ex{i}", [128, IB * 128], BF16) for i in range(2)]
    mt = [sb(f"d_mt{i}", [128, IB * 128], BF16) for i in range(2)]
    ub = [sb(f"d_ub{i}", [128, 8, NB], BF16) for i in range(2)]
    vb = [sb(f"d_vb{i}", [128, NB // 128, 1024], BF16) for i in range(2)]
    ga = [sb(f"d_ga{i}", [128, NB], BF16) for i in range(2)]
    Wm = [sb(f"d_W{i}", [128, NB], BF16) for i in range(2)]
    WT = [sb(f"d_WT{i}", [128, NB // 128, 128], BF16) for i in range(2)]
    acc = [ps(f"d_acc{t}", [128, 1024], F32) for t in range(TG)]
    Ap = [ps(f"d_Ap{i}", [128, 512], F32) for i in range(2)]
    WTp = ps("d_WTp", [128, 1024], BF16)
    misc = ps("d_misc", [128, 512], F32)
    miscb = misc[:].bitcast(BF16)
    mixT_v = c.mixT_s.rearrange("(cc p) t -> p cc t", p=128)
    uTb_v = c.uT_bf.rearrange("(kc p) n -> p kc n", p=128)
    vb_v = c.v_bf.rearrange("(a p) d -> p a d", p=128)
    NCH = NB // 128
    ngroups = ngroups if ngroups is not None else NQ // 128 // TG
    blk = 0
    for g in range(ngroups):
        for ti in range(TG):
            tok0 = (g * TG + ti) * 128
            P.dma(mx[:], mixT_v[:, :, tok0:tok0 + 128], w=["mx"], sem="mx")
            P.dma(xt[:], c.xtok[tok0:tok0 + 128, :], w=["xt"], sem="xt")
            mp = acc[ti]
            for half in range(2):
                for cc in range(8):
                    P.op("pe", lambda e, half=half, cc=cc, mp=mp: e.matmul(mp[:, half * 512:(half + 1) * 512], mx[:, cc, :],
                                                                          wo[:, cc, half * 512:(half + 1) * 512], start=(cc == 0), stop=(cc == 7)),
                         r=["mx", WKEYS[0]], w=[f"acc{ti}"])
            for half in range(2):
                hs = slice(half * 512, (half + 1) * 512)
                P.op("dve", lambda e, mp=mp, ti=ti, hs=hs: e.scalar_tensor_tensor(out=x1[ti][:, hs], in0=xt[:, hs], scalar=ALPHA, in1=mp[:, hs], op0=ALU.mult, op1=ALU.add),
                     r=["xt", f"acc{ti}"], w=[f"x1_{ti}"])
            layernorm_rows(ph, "ln1", x1[ti][:], lnp[:, 0, :], lnp[:, 1, :], x1[ti][:], [f"x1_{ti}"], f"x1_{ti}")
            P.op("act", lambda e, ti=ti: e.activation(out=x1b[:], in_=x1[ti][:], func=AF.Copy), r=[f"x1_{ti}"], w=["x1b"])
            for kc in range(8):
                P.op("pe", lambda e, kc=kc: e.transpose(out=WTp[:, kc * 128:(kc + 1) * 128], in_=x1b[:, kc * 128:(kc + 1) * 128], identity=idb[:]),
                     r=["x1b", "idb"], w=["WTp"])
            P.op("dve", lambda e, ti=ti: e.tensor_copy(out=x1T[ti][:], in_=WTp[:].rearrange("p (k t) -> p k t", k=8)), r=["WTp"], w=[f"x1T{ti}", "WTp"])
            for qg in range(4):
                ws = wqn % 2
                wqn += 1
                P.dma(wqs[ws][:], wqb_v[:, :, qg * 512:(qg + 1) * 512], w=[f"wqs{ws}"], sem=f"wqs{ws}")
                for j in range(4):
                    for kc in range(8):
                        P.op("pe", lambda e, j=j, kc=kc, ti=ti, ws=ws: e.matmul(misc[:, j * 128:(j + 1) * 128], wqs[ws][:, kc, j * 128:(j + 1) * 128],
                                                                                x1T[ti][:, kc, :], start=(kc == 0), stop=(kc == 7)),
                             r=[f"wqs{ws}", f"x1T{ti}"], w=["misc"])
                P.op("act", lambda e, qg=qg: e.activation(out=qT[:, qg * 4:(qg + 1) * 4, :], in_=misc[:].rearrange("p (j t) -> p j t", j=4), func=AF.Copy),
                     r=["misc"], w=[f"qT{qg}", "misc"])
            for qg in range(4):
                for j in range(4):
                    ch = qg * 4 + j
                    hh, half = ch // 2, ch % 2
                    P.op("pe", lambda e, ch=ch, j=j, hh=hh, half=half: e.matmul(misc[:, j * 128:(j + 1) * 128], qT[:, ch, :], kT[half][:, hh, :],
                                                                                start=True, stop=True),
                         r=[f"qT{qg}", f"kT{half}"], w=["misc"])
                P.op("dve", lambda e, qg=qg: e.tensor_copy(out=s12[:, qg * 4:(qg + 1) * 4, :], in_=misc[:].rearrange("p (j t) -> p j t", j=4)),
                     r=["misc"], w=[f"s12_{qg}", "misc"])
            SK = [f"s12_{q}" for q in range(4)]
            for ch in range(16):
                P.op("dve", lambda e, ch=ch: e.max(out=t16[:, ch, 0:8], in_=s12[:, ch, :]), r=SK, w=[f"t16a{ch}"])
                P.op("dve", lambda e, ch=ch: e.match_replace(out=wk[:, 0:128], in_to_replace=t16[:, ch, 0:8], in_values=s12[:, ch, :], imm_value=-1e30),
                     r=SK + [f"t16a{ch}"], w=["wk"])
                P.op("dve", lambda e, ch=ch: e.max(out=t16[:, ch, 8:16], in_=wk[:, 0:128]), r=["wk"], w=[f"t16b{ch}"])
            TK = [f"t16a{ch}" for ch in range(16)] + [f"t16b{ch}" for ch in range(16)]
            in0 = mkap(t16[:, 0, :], [[32, 8], [1, 16], [0, 16]])
            in1 = mkap(t16[:, 1, :], [[32, 8], [0, 16], [1, 16]])
            P.op("pool", lambda e, in0=in0, in1=in1: e.tensor_tensor(out=cand[:].rearrange("p h (a b) -> p h a b", a=16), in0=in0, in1=in1, op=ALU.add),
                 r=TK, w=["cand"])
            for hh in range(8):
                P.op("dve", lambda e, hh=hh: e.max(out=b16[:, hh, 0:8], in_=cand[:, hh, :]), r=["cand"], w=[f"b16a{hh}"])
                P.op("dve", lambda e, hh=hh: e.match_replace(out=wk[:], in_to_replace=b16[:, hh, 0:8], in_values=cand[:, hh, :], imm_value=-1e30),
                     r=["cand", f"b16a{hh}"], w=["wk"])
                P.op("dve", lambda e, hh=hh: e.max(out=b16[:, hh, 8:16], in_=wk[:]), r=["wk"], w=[f"b16b{hh}"])
            BK = [f"b16a{hh}" for hh in range(8)] + [f"b16b{hh}" for hh in range(8)]
            m_b = mkap(b16[:, 0, 0:1], [[16, 8], [0, 16]])
            P.op("pool", lambda e, m_b=m_b: e.tensor_tensor(out=dd[:], in0=b16[:], in1=m_b, op=ALU.subtract), r=BK, w=["dd"])
            P.op("act", lambda e: e.activation(out=ee[:], in_=dd[:], func=AF.Exp), r=["dd"], w=["ee"])
            P.op("dve", lambda e: e.tensor_reduce(out=zz[:], in_=ee[:], axis=AX.X, op=ALU.add), r=["ee"], w=["zz"])
            P.op("act", lambda e: e.activation(out=lz[:], in_=zz[:], func=AF.Ln), r=["zz"], w=["lz"])
            m_v = mkap(b16[:, 0, 0:1], [[16, 8]])
            t_v = mkap(b16[:, 0, 15:16], [[16, 8]])
            P.op("pool", lambda e, m_v=m_v: e.tensor_tensor(out=mz[:], in0=m_v, in1=lz[:], op=ALU.add), r=BK + ["lz"], w=["mz"])
            P.op("pool", lambda e, t_v=t_v: e.tensor_tensor(out=th[:], in0=t_v, in1=mz[:], op=ALU.subtract), r=BK + ["mz"], w=["th"])
            s1_v = mkap(s12[:, 0, :], [[256, 8], [1, 128]])
            mz_b = mkap(mz[:, 0:1], [[1, 8], [0, 128]])
            P.op("pool", lambda e, s1_v=s1_v, mz_b=mz_b: e.tensor_tensor(out=a1[:], in0=s1_v, in1=mz_b, op=ALU.subtract), r=SK + ["mz"], w=["a1"])
            for ib in range(128 // IB):
                for hh in range(8):
                    bs = blk % 2
                    blk += 1
                    a_b = mkap(a1[:, hh, ib * IB:(ib + 1) * IB], [[1, IB], [0, 128]])
                    s_b = mkap(s12[:, 2 * hh + 1, :], [[0, IB], [1, 128]])
                    P.op("pool", lambda e, a_b=a_b, s_b=s_b, bs=bs: e.tensor_tensor(out=xx[bs][:].rearrange("p (i j) -> p i j", i=IB), in0=a_b, in1=s_b, op=ALU.add),
                         r=["a1"] + SK, w=[f"xx{bs}"])
                    P.op("act", lambda e, bs=bs: e.activation(out=ex[bs][:], in_=xx[bs][:], func=AF.Exp), r=[f"xx{bs}"], w=[f"ex{bs}"])
                    gsl = G[ti][:, ib * IB * 128:(ib + 1) * IB * 128]
                    if hh == 0:
                        P.op("dve", lambda e, bs=bs, hh=hh, gsl=gsl: e.scalar_tensor_tensor(out=gsl, in0=xx[bs][:], scalar=th[:, hh:hh + 1], in1=ex[bs][:],
                                                                                           op0=ALU.is_ge, op1=ALU.mult),
                             r=[f"xx{bs}", f"ex{bs}", "th"], w=[f"G{ti}_{ib}"])
                    else:
                        P.op("dve", lambda e, bs=bs, hh=hh: e.scalar_tensor_tensor(out=mt[bs][:], in0=xx[bs][:], scalar=th[:, hh:hh + 1], in1=ex[bs][:],
                                                                                  op0=ALU.is_ge, op1=ALU.mult),
                             r=[f"xx{bs}", f"ex{bs}", "th"], w=[f"mt{bs}"])
                        P.op("dve", lambda e, bs=bs, gsl=gsl: e.tensor_tensor(out=gsl, in0=gsl, in1=mt[bs][:], op=ALU.add),
                             r=[f"mt{bs}", f"G{ti}_{ib}"], w=[f"G{ti}_{ib}"])
        nblk = 16384 // NB
        for nb in range(nblk):
            s = nb % 2
            P.dma(ub[s][:], uTb_v[:, :, nb * NB:(nb + 1) * NB], w=[f"ub{s}"], sem=f"ub{s}")
            P.dma(vb[s][:], vb_v[:, nb * NCH:(nb + 1) * NCH, :], w=[f"vb{s}"], sem=f"vb{s}")
            for ti in range(TG):
                k = (nb * TG + ti) % 2
                for kc in range(8):
                    P.op("pe", lambda e, k=k, kc=kc, ti=ti, s=s: e.matmul(Ap[k][:, 0:NB], x1T[ti][:, kc, :], ub[s][:, kc, :], start=(kc == 0), stop=(kc == 7)),
                         r=[f"x1T{ti}", f"ub{s}"], w=[f"Ap{k}"])
                P.op("act", lambda e, k=k: e.activation(out=ga[k][:], in_=Ap[k][:, 0:NB], func=AF.Gelu), r=[f"Ap{k}"], w=[f"ga{k}"])
                gkeys = [f"G{ti}_{ib}" for ib in range((nb * NB) // (IB * 128), ((nb + 1) * NB - 1) // (IB * 128) + 1)]
                P.op("dve", lambda e, k=k, ti=ti, nb=nb: e.tensor_tensor(out=Wm[k][:], in0=ga[k][:], in1=G[ti][:, nb * NB:(nb + 1) * NB], op=ALU.mult),
                     r=[f"ga{k}"] + gkeys, w=[f"W{k}"])
                wtp = WTp if k == 0 else miscb
                wkey = "WTp" if k == 0 else "misc"
                for ch in range(NCH):
                    P.op("pe", lambda e, k=k, ch=ch, wtp=wtp: e.transpose(out=wtp[:, ch * 128:(ch + 1) * 128], in_=Wm[k][:, ch * 128:(ch + 1) * 128], identity=idb[:]),
                         r=[f"W{k}", "idb"], w=[wkey])
                P.op("act", lambda e, k=k, wtp=wtp: e.activation(out=WT[k][:], in_=wtp[:, 0:NCH * 128].rearrange("p (c t) -> p c t", c=NCH), func=AF.Copy),
                     r=[wkey], w=[f"WT{k}", wkey])
                for ch in range(NCH):
                    for half in range(2):
                        P.op("pe", lambda e, k=k, ch=ch, half=half, ti=ti, s=s, nb=nb: e.matmul(acc[ti][:, half * 512:(half + 1) * 512], WT[k][:, ch, :],
                                                                                                  vb[s][:, ch, half * 512:(half + 1) * 512],
                                                                                                  start=(nb == 0 and ch == 0), stop=(nb == nblk - 1 and ch == NCH - 1)),
                             r=[f"WT{k}", f"vb{s}"], w=[f"acc{ti}"])
        for ti in range(TG):
            tok0 = (g * TG + ti) * 128
            for half in range(2):
                hs = slice(half * 512, (half + 1) * 512)
                P.op("dve", lambda e, ti=ti, hs=hs: e.scalar_tensor_tensor(out=xt[:, hs], in0=x1[ti][:, hs], scalar=ALPHA, in1=acc[ti][:, hs], op0=ALU.mult, op1=ALU.add),
                     r=[f"x1_{ti}", f"acc{ti}"], w=["xt"])
            layernorm_rows(ph, "ln2", xt[:], lnp[:, 2, :], lnp[:, 3, :], xt[:], ["xt"], "xt")
            ph.store(c.y[tok0:tok0 + 128, :], xt[:], r=["xt"], sem="yo")
    ph.finish()


def ln_inplace(ph, buf, lnp, key):
    P = ph.P
    st, mv, sd, rs = ph.t["st"], ph.t["mv"], ph.t["sd"], ph.t["rs"]
    P.op("dve", lambda e: e.bn_stats(out=st[:, 0, :], in_=buf[:, 0:512]), r=[key], w=["st0"])
    P.op("dve", lambda e: e.bn_stats(out=st[:, 1, :], in_=buf[:, 512:1024]), r=[key], w=["st1"])
    P.op("dve", lambda e: e.bn_aggr(out=mv[:], in_=st[:].rearrange("p a b -> p (a b)")), r=["st0", "st1"], w=["mv"])
    P.op("act", lambda e: e.activation(out=sd[:], in_=mv[:, 1:2], func=AF.Sqrt, bias=LN_EPS, scale=1.0), r=["mv"], w=["sd"])
    P.op("dve", lambda e: e.reciprocal(out=rs[:], in_=sd[:]), r=["sd"], w=["rs"])
    P.op("dve", lambda e: e.tensor_scalar(out=buf, in0=buf, scalar1=mv[:, 0:1], scalar2=rs[:, 0:1], op0=ALU.subtract, op1=ALU.mult),
         r=[key, "mv", "rs"], w=[key])
    P.op("dve", lambda e: e.tensor_tensor(out=buf, in0=buf, in1=lnp[:, 0, :], op=ALU.mult), r=[key, "lnp"], w=[key])
    P.op("dve", lambda e: e.tensor_tensor(out=buf, in0=buf, in1=lnp[:, 1, :], op=ALU.add), r=[key, "lnp"], w=[key])


def phase_D2(nc, c, ngroups=None):
    TG = 2
    ph = Phase(nc)
    P = ph.P
    sb, ps = ph.sb, ph.ps
    ph.t = {}
    arena = sb("d_arena", [128, 4096], F32)
    ar2 = arena[:, 2048:4096].bitcast(BF16)
    cand = arena[:, 0:2048].rearrange("p (h c) -> p h c", h=8)
    y200 = [arena[:, s * 1024:(s + 1) * 1024] for s in range(2)]
    qT = ar2[:, 0:2048].rearrange("p (c t) -> p c t", c=16)
    mx = ar2[:, 2048:3072].rearrange("p (c t) -> p c t", c=8)
    x1b = ar2[:, 3072:4096]
    OHc = [ar2[:, s * 2048:(s + 1) * 2048] for s in range(2)]
    idf = sb("d_idf", [128, 128], F32)
    idb = sb("d_idb", [128, 128], BF16)
    P.dma(idf[:], c.ident[:, :], w=["idf"], sem="idf")
    cast(P, "dve", idb[:], idf[:], r=["idf"], w=["idb"])
    kT = [sb(f"d_k{j}T", [128, 8, 128], BF16) for j in range(2)]
    for j, src in enumerate((c.k1T, c.k2T)):
        sv = arena[:, 0:1024].rearrange("p (h n) -> p h n", h=8)
        P.dma(sv, src[:, :, :], w=["arena"], sem="stage")
        cast(P, "dve", kT[j][:], sv, r=["arena"], w=[f"kT{j}", "arena"])
    ring = [sb(f"d_ring{i}", [128, 8, 256], BF16) for i in range(2)]
    lnp = sb("d_lnp", [128, 2, 1024], F32)
    ph.t["st"] = sb("d_st", [128, 2, 6], F32)
    ph.t["mv"] = sb("d_mv", [128, 2], F32)
    ph.t["sd"] = sb("d_sd", [128, 1], F32)
    ph.t["rs"] = sb("d_rs", [128, 1], F32)
    xt = sb("d_xt", [128, 1024], F32)
    x1T = sb("d_x1T", [128, 8, TG * 128], BF16)
    s12 = sb("d_s12", [128, 16, 128], F32)
    wk = sb("d_wk", [128, 256], F32)
    t16 = sb("d_t16", [128, 16, 16], F32)
    b16 = sb("d_b16", [128, 8, 16], F32)
    dd = sb("d_dd", [128, 8, 16], F32)
    ee = sb("d_ee", [128, 8, 16], F32)
    zz = sb("d_zz", [128, 8], F32)
    lz = sb("d_lz", [128, 8], F32)
    mz = sb("d_mz", [128, 8], F32)
    th = sb("d_th", [128, 8], F32)
    cex = sb("d_cex", [128, 8], F32)
    cT = sb("d_cT", [128, 128], F32)
    thr2 = sb("d_thr2", [128, 8], F32)
    bb2 = sb("d_bb2", [128, 8, 16], F32)
    msk = [sb(f"d_msk{i}", [128, 1024], BF16) for i in range(2)]
    eex = [sb(f"d_eex{i}", [128, 1024], BF16) for i in range(2)]
    Mc = [sb(f"d_Mc{i}", [128, 1024], BF16) for i in range(2)]
    OHT = sb("d_OHT", [128, 128, 128], BF16)
    MT = sb("d_MT", [128, 128, 64], BF16)
    G = sb("d_G", [128, TG * 128, 128], BF16)
    ub = [sb(f"d_ub{i}", [128, 8, 256], BF16) for i in range(3)]
    vb = [sb(f"d_vb{i}", [128, 2, 1024], BF16) for i in range(3)]
    ga = [sb(f"d_ga{i}", [128, TG * 128], BF16) for i in range(2)]
    Wj = [sb(f"d_Wj{i}", [128, TG * 128], BF16) for i in range(2)]
    acc = [ps(f"d_acc{t}", [128, 1024], F32) for t in range(TG)]
    Ap = [ps(f"d_Ap{i}", [128, 512], F32) for i in range(2)]
    WTp = ps("d_WTp", [128, 1024], BF16)
    misc = ps("d_misc", [128, 512], F32)
    Apb = [Ap[i][:].bitcast(BF16) for i in range(2)]
    miscb = misc[:].bitcast(BF16)
    mixT_v = c.mixT_s.rearrange("(cc p) t -> p cc t", p=128)
    uTb_v = c.uT_bf.rearrange("(kc p) n -> p kc n", p=128)
    vb_v = c.v_bf.rearrange("(a p) d -> p a d", p=128)

    ngroups = ngroups if ngroups is not None else NQ // 128 // TG
    preloaded = set()
    epb = [arena[:, 0:1024], arena[:, 1024:2048]]
    rn = 0
    SK = [f"s12_{q}" for q in range(4)]
    TK = [f"t16a{ch}" for ch in range(16)] + [f"t16b{ch}" for ch in range(16)]
    BK = [f"b16a{hh}" for hh in range(8)] + [f"b16b{hh}" for hh in range(8)]
    och = 0
    mch = 0
    gev = 0
    for g in range(ngroups):
        for ti in range(TG):
            tok0 = (g * TG + ti) * 128
            if tok0 not in preloaded:
                P.dma(mx, mixT_v[:, :, tok0:tok0 + 128], w=["mx", "arena"], sem="mx")
                P.dma(xt[:], c.xtok[tok0:tok0 + 128, :], w=["xt"], sem="xt")
            P.dma(lnp[:, 0, :], c.ln1_g[0:1, :].partition_broadcast(128), w=["lnp"], sem="lnp")
            P.dma(lnp[:, 1, :], c.ln1_b[0:1, :].partition_broadcast(128), w=["lnp"], sem="lnp")
            mp = acc[ti]
            for dq in range(4):
                rs_ = rn % 2
                rn += 1
                P.dma(ring[rs_][:], c.wo_bf[dq], w=[f"ring{rs_}"], sem=f"ring{rs_}")
                for cc in range(8):
                    P.op("pe", lambda e, dq=dq, cc=cc, mp=mp, rs_=rs_: e.matmul(mp[:, dq * 256:(dq + 1) * 256], mx[:, cc, :], ring[rs_][:, cc, :],
                                                                                start=(cc == 0), stop=(cc == 7)),
                         r=["mx", f"ring{rs_}", f"acc{ti}_0", f"acc{ti}_1"], w=[f"acc{ti}"])
            for half in range(2):
                hs = slice(half * 512, (half + 1) * 512)
                P.op("dve", lambda e, mp=mp, hs=hs: e.scalar_tensor_tensor(out=xt[:, hs], in0=xt[:, hs], scalar=ALPHA, in1=mp[:, hs], op0=ALU.mult, op1=ALU.add),
                     r=["xt", f"acc{ti}"], w=["xt", f"acc{ti}"])
            ln_inplace(ph, xt[:], lnp, "xt")
            ph.stores.append(P.dma(c.x1_s[tok0:tok0 + 128, :], xt[:], r=["xt"], w=[f"x1s{ti}"], sem="x1s"))
            P.op("act", lambda e: e.activation(out=x1b, in_=xt[:], func=AF.Copy), r=["xt"], w=["x1b", "arena"])
            for kc in range(8):
                P.op("pe", lambda e, kc=kc: e.transpose(out=WTp[:, kc * 128:(kc + 1) * 128], in_=x1b[:, kc * 128:(kc + 1) * 128], identity=idb[:]),
                     r=["x1b", "idb"], w=["WTp"])
            P.op("dve", lambda e, ti=ti: e.tensor_copy(out=x1T[:, :, ti * 128:(ti + 1) * 128], in_=WTp[:].rearrange("p (k t) -> p k t", k=8)),
                 r=["WTp"], w=[f"x1T{ti}", "WTp"])
            for qg in range(8):
                rs_ = rn % 2
                rn += 1
                P.dma(ring[rs_][:], c.wq_bf[qg], w=[f"ring{rs_}"], sem=f"ring{rs_}")
                for j in range(2):
                    for kc in range(8):
                        P.op("pe", lambda e, j=j, kc=kc, ti=ti, rs_=rs_: e.matmul(misc[:, j * 128:(j + 1) * 128], ring[rs_][:, kc, j * 128:(j + 1) * 128],
                                                                                  x1T[:, kc, ti * 128:(ti + 1) * 128], start=(kc == 0), stop=(kc == 7)),
                             r=[f"ring{rs_}", f"x1T{ti}"], w=["misc"])
                P.op("act", lambda e, qg=qg: e.activation(out=qT[:, qg * 2:(qg + 1) * 2, :], in_=misc[:, 0:256].rearrange("p (j t) -> p j t", j=2), func=AF.Copy),
                     r=["misc"], w=[f"qT{qg}", "misc", "arena"])
            for qg in range(4):
                for j in range(4):
                    ch = qg * 4 + j
                    hh, half = ch // 2, ch % 2
                    P.op("pe", lambda e, ch=ch, j=j, hh=hh, half=half: e.matmul(misc[:, j * 128:(j + 1) * 128], qT[:, ch, :], kT[half][:, hh, :],
                                                                                start=True, stop=True),
                         r=[f"qT{ch // 2}", f"kT{half}"], w=["misc"])
                P.op("dve", lambda e, qg=qg: e.tensor_copy(out=s12[:, qg * 4:(qg + 1) * 4, :], in_=misc[:].rearrange("p (j t) -> p j t", j=4)),
                     r=["misc"], w=[f"s12_{qg}", "misc"])
            for c2 in range(8):
                pair = (2 * c2, 2 * c2 + 1)
                for w_, ch in enumerate(pair):
                    P.op("dve", lambda e, ch=ch: e.max(out=t16[:, ch, 0:8], in_=s12[:, ch, :]), r=SK, w=[f"t16a{ch}"])
                for w_, ch in enumerate(pair):
                    P.op("dve", lambda e, ch=ch, w_=w_: e.match_replace(out=wk[:, w_ * 128:(w_ + 1) * 128], in_to_replace=t16[:, ch, 0:8], in_values=s12[:, ch, :], imm_value=-1e30),
                         r=SK + [f"t16a{ch}"], w=[f"wk{w_}"])
                for w_, ch in enumerate(pair):
                    P.op("dve", lambda e, ch=ch, w_=w_: e.max(out=t16[:, ch, 8:16], in_=wk[:, w_ * 128:(w_ + 1) * 128]), r=[f"wk{w_}"], w=[f"t16b{ch}"])
            in0 = mkap(t16[:, 0, :], [[32, 8], [1, 16], [0, 16]])
            in1 = mkap(t16[:, 1, :], [[32, 8], [0, 16], [1, 16]])
            P.op("dve", lambda e, in0=in0, in1=in1: e.tensor_tensor(out=cand.rearrange("p h (a b) -> p h a b", a=16), in0=in0, in1=in1, op=ALU.add),
                 r=TK, w=["cand", "arena", "epb0", "epb1"])
            for hh in range(8):
                P.op("dve", lambda e, hh=hh: e.max(out=b16[:, hh, 0:8], in_=cand[:, hh, :]), r=["cand"], w=[f"b16a{hh}"])
                P.op("dve", lambda e, hh=hh: e.match_replace(out=wk[:], in_to_replace=b16[:, hh, 0:8], in_values=cand[:, hh, :], imm_value=-1e30),
                     r=["cand", f"b16a{hh}"], w=["wk0", "wk1"])
                P.op("dve", lambda e, hh=hh: e.max(out=b16[:, hh, 8:16], in_=wk[:]), r=["wk0", "wk1"], w=[f"b16b{hh}"])
            m_b = mkap(b16[:, 0, 0:1], [[16, 8], [0, 16]])
            P.op("pool", lambda e, m_b=m_b: e.tensor_tensor(out=dd[:], in0=b16[:], in1=m_b, op=ALU.subtract), r=BK, w=["dd"])
            P.op("act", lambda e: e.activation(out=ee[:], in_=dd[:], func=AF.Exp), r=["dd"], w=["ee"])
            P.op("dve", lambda e: e.tensor_reduce(out=zz[:], in_=ee[:], axis=AX.X, op=ALU.add), r=["ee"], w=["zz"])
            P.op("act", lambda e: e.activation(out=lz[:], in_=zz[:], func=AF.Ln), r=["zz"], w=["lz"])
            m_v = mkap(b16[:, 0, 0:1], [[16, 8]])
            t_v = mkap(b16[:, 0, 15:16], [[16, 8]])
            P.op("pool", lambda e, m_v=m_v: e.tensor_tensor(out=mz[:], in0=m_v, in1=lz[:], op=ALU.add), r=BK + ["lz"], w=["mz"])
            P.op("pool", lambda e, t_v=t_v: e.tensor_tensor(out=th[:], in0=t_v, in1=mz[:], op=ALU.subtract), r=BK + ["mz"], w=["th"])
            P.op("pool", lambda e: e.tensor_scalar(out=thr2[:], in0=mz[:], scalar1=-200.0, scalar2=None, op0=ALU.add), r=["mz"], w=["thr2"])
            P.op("pool", lambda e: e.tensor_scalar(out=cex[:], in0=th[:], scalar1=200.0, scalar2=None, op0=ALU.add), r=["th"], w=["cex"])
            v1_v = mkap(t16[:, 0, :], [[32, 8], [1, 16]])
            P.op("pool", lambda e, v1_v=v1_v: e.tensor_tensor(out=bb2[:], in0=v1_v, in1=mkap(thr2[:, 0:1], [[1, 8], [0, 16]]), op=ALU.subtract),
                 r=TK + ["thr2"], w=["bb2"])
            def oh_a(ic):
                s_ = ic % 2
                s1_b = mkap(s12[:, 0, ic * 16:(ic + 1) * 16], [[256, 8], [0, 16], [1, 16]])
                v1_b = mkap(t16[:, 0, :], [[32, 8], [1, 16], [0, 16]])
                P.op("dve", lambda e: e.tensor_tensor(out=OHc[s_].rearrange("p (h r i) -> p h r i", h=8, r=16), in0=s1_b, in1=v1_b, op=ALU.is_equal),
                     r=SK + TK + ["arena"], w=[f"OHc{s_}"])

            def oh_b(ic):
                s_ = ic % 2
                o3 = OHc[s_].rearrange("p (q i) -> p q i", i=16)
                for ii in range(16):
                    bk = ii // 8
                    P.op("pe", lambda e, ii=ii, bk=bk: e.transpose(out=Apb[bk][:, (ii % 8) * 128:(ii % 8 + 1) * 128], in_=o3[:, :, ii], identity=idb[:]),
                         r=[f"OHc{s_}", "idb"], w=[f"Ap{bk}"])
                for bk in range(2):
                    i0 = ic * 16 + bk * 8
                    ov = OHT[:, :, i0:i0 + 8]
                    P.op("act", lambda e, bk=bk, ov=ov: e.activation(out=ov, in_=mkap(Apb[bk][:, 0:1], [[1, 128], [128, 8]]), func=AF.Copy),
                         r=[f"Ap{bk}"], w=["OHT", f"Ap{bk}"])

            def m_a(jh, jc):
                s_ = jc % 2
                j0 = jh * 64 + jc * 8
                s2_b = mkap(s12[:, 1, j0:j0 + 8], [[256, 8], [0, 16], [1, 8]])
                bb_b = mkap(bb2[:, 0, :], [[16, 8], [1, 16], [0, 8]])
                P.op("pool", lambda e: e.tensor_tensor(out=y200[s_].rearrange("p (h r j) -> p h r j", h=8, r=16), in0=s2_b, in1=bb_b, op=ALU.add),
                     r=SK + ["bb2", "arena"], w=[f"y200{s_}"])
                P.op("dve", lambda e: e.tensor_tensor(out=msk[s_][:].rearrange("p (h q) -> p h q", h=8), in0=y200[s_].rearrange("p (h q) -> p h q", h=8),
                                                      in1=mkap(cex[:, 0:1], [[1, 8], [0, 128]]), op=ALU.is_ge),
                     r=[f"y200{s_}", "cex"], w=[f"msk{s_}"])
                P.op("act", lambda e: e.activation(out=eex[s_][:], in_=y200[s_], func=AF.Exp, bias=-200.0, scale=1.0),
                     r=[f"y200{s_}"], w=[f"eex{s_}"])
                P.op("dve", lambda e: e.tensor_tensor(out=Mc[s_][:], in0=msk[s_][:], in1=eex[s_][:], op=ALU.mult),
                     r=[f"msk{s_}", f"eex{s_}"], w=[f"Mc{s_}"])

            def m_b(jh, jc):
                s_ = jc % 2
                m3 = Mc[s_][:].rearrange("p (q j) -> p q j", j=8)
                tb = WTp if s_ == 0 else miscb
                tkey = "WTp" if s_ == 0 else "misc"
                for jj in range(8):
                    P.op("pe", lambda e, jj=jj: e.transpose(out=tb[:, jj * 128:(jj + 1) * 128], in_=m3[:, :, jj], identity=idb[:]),
                         r=[f"Mc{s_}", "idb"], w=[tkey])
                mv_ = MT[:, :, jc * 8:jc * 8 + 8]
                P.op("act", lambda e: e.activation(out=mv_, in_=mkap(tb[:, 0:1], [[1, 128], [128, 8]]), func=AF.Copy),
                     r=[tkey], w=["MT", tkey])

            def scatter(jh):
                nonlocal gev
                gacc = acc[1 - ti]
                for t8 in range(16):
                    bk = gev % 2
                    gev += 1
                    for tt in range(8):
                        t = t8 * 8 + tt
                        P.op("pe", lambda e, t=t, tt=tt, bk=bk: e.matmul(gacc[:, bk * 512 + tt * 64: bk * 512 + (tt + 1) * 64], OHT[:, t, :], MT[:, t, :],
                                                                        start=True, stop=True),
                             r=["OHT", "MT", f"acc{1 - ti}"], w=[f"acc{1 - ti}_{bk}"])
                    gv = mkap(G[:, ti * 128 + t8 * 8, jh * 64:(jh + 1) * 64], [[128, 8], [1, 64]])
                    gsrc = gacc[:, bk * 512:(bk + 1) * 512].rearrange("p (t j) -> p t j", t=8)
                    if t8 % 2 == 0:
                        P.op("act", lambda e, gv=gv, gsrc=gsrc: e.activation(out=gv, in_=gsrc, func=AF.Copy),
                             r=[f"acc{1 - ti}_{bk}"], w=[f"G{ti}{jh}", f"acc{1 - ti}_{bk}"])
                    else:
                        P.op("dve", lambda e, gv=gv, gsrc=gsrc: e.tensor_copy(out=gv, in_=gsrc),
                             r=[f"acc{1 - ti}_{bk}"], w=[f"G{ti}{jh}", f"acc{1 - ti}_{bk}"])

            oh_a(0)
            m_a(0, 0)
            for step in range(8):
                if step + 1 < 8:
                    oh_a(step + 1)
                    m_a(0, step + 1)
                oh_b(step)
                m_b(0, step)
            m_a(1, 0)
            scatter(0)
            for step in range(8):
                if step + 1 < 8:
                    m_a(1, step + 1)
                m_b(1, step)
            scatter(1)
        GK = [f"G{ti}{jh}" for ti in range(TG) for jh in range(2)]
        AK = [f"acc{t}_{b}" for t in range(TG) for b in range(2)]
        for ti in range(TG):
            tok0 = (g * TG + ti) * 128
            P.dma(epb[ti], c.x1_s[tok0:tok0 + 128, :], r=[f"x1s{ti}"], w=[f"epb{ti}", "cand", "y2000", "y2001", "arena"], sem=f"epb{ti}")
        P.dma(lnp[:, 0, :], c.ln2_g[0:1, :].partition_broadcast(128), w=["lnp"], sem="lnp")
        P.dma(lnp[:, 1, :], c.ln2_b[0:1, :].partition_broadcast(128), w=["lnp"], sem="lnp")
        NSL = 3

        def load_uv(jb):
            sl = jb % NSL
            P.dma(ub[sl][:], uTb_v[:, :, jb * 256:(jb + 1) * 256], w=[f"ub{sl}"], sem=f"ub{sl}")
            P.dma(vb[sl][:], vb_v[:, jb * 2:(jb + 1) * 2, :], w=[f"vb{sl}"], sem=f"vb{sl}")

        def issue_A(j):
            jb, jj = j // 2, j % 2
            sl, k = jb % NSL, j % 2
            if jj == 0 and jb + 1 < 64:
                load_uv(jb + 1)
            for kc in range(8):
                P.op("pe", lambda e, kc=kc: e.matmul(Ap[k][:, 0:TG * 128], ub[sl][:, kc, jj * 128:(jj + 1) * 128], x1T[:, kc, :],
                                                     start=(kc == 0), stop=(kc == 7)),
                     r=[f"x1T{t}" for t in range(TG)] + [f"ub{sl}"], w=[f"Ap{k}"])
            P.op("act", lambda e: e.activation(out=ga[k][:], in_=Ap[k][:, 0:TG * 128], func=AF.Gelu), r=[f"Ap{k}"], w=[f"ga{k}", f"Ap{k}"])
            gj = mkap(G[:, 0, j:j + 1], [[128, TG * 128]])
            P.op("dve", lambda e: e.tensor_tensor(out=Wj[k][:], in0=ga[k][:], in1=gj, op=ALU.mult),
                 r=[f"ga{k}"] + GK, w=[f"Wj{k}"])

        load_uv(0)
        issue_A(0)
        for j in range(128):
            if j + 1 < 128:
                issue_A(j + 1)
            jb, jj = j // 2, j % 2
            sl, k = jb % NSL, j % 2
            for ti in range(TG):
                for half in range(2):
                    P.op("pe", lambda e, k=k, ti=ti, half=half, sl=sl, jj=jj, j=j: e.matmul(acc[ti][:, half * 512:(half + 1) * 512], Wj[k][:, ti * 128:(ti + 1) * 128],
                                                                                             vb[sl][:, jj, half * 512:(half + 1) * 512],
                                                                                             start=(j == 0), stop=(j == 127)),
                         r=[f"Wj{k}", f"vb{sl}"] + ([f"acc{ti}_0", f"acc{ti}_1"] if j == 0 else []), w=[f"acc{ti}"])
        if g + 1 < ngroups:
            ntok = (g + 1) * TG * 128
            P.dma(mx, mixT_v[:, :, ntok:ntok + 128], w=["mx", "arena"], sem="mx")
            P.dma(xt[:], c.xtok[ntok:ntok + 128, :], w=["xt"], sem="xt")
            preloaded.add(ntok)
        for ti in range(TG):
            tok0 = (g * TG + ti) * 128
            eb = epb[ti]
            ek = f"epb{ti}"
            for half in range(2):
                hs = slice(half * 512, (half + 1) * 512)
                P.op("dve", lambda e, ti=ti, hs=hs, eb=eb: e.scalar_tensor_tensor(out=eb[:, hs], in0=eb[:, hs], scalar=ALPHA, in1=acc[ti][:, hs], op0=ALU.mult, op1=ALU.add),
                     r=[ek, f"acc{ti}"], w=[ek, f"acc{ti}"])
            ln_inplace(ph, eb, lnp, ek)
            ph.stores.append(P.dma(c.y[tok0:tok0 + 128, :], eb, r=[ek], w=[], sem="yo"))
    ph.finish()


def build_program():
    nc = bass.Bass("TRN2", target_bir_lowering=False)
    c = declare(nc)
    phase_A(nc, c)
    phase_A3(nc, c)
    phase_B(nc, c)
    phase_D2(nc, c)
    return nc


def _rope_table(pos):
    inv = (1.0 / (np.float32(10000.0) ** (np.arange(0, 32, 2, dtype=np.float32) / np.float32(32)))).astype(np.float32)
    ang = pos.astype(np.float32)[:, None] * inv[None, :]
    cs, sn = np.cos(ang).astype(np.float32), np.sin(ang).astype(np.float32)
    return np.ascontiguousarray(np.concatenate([cs, cs, -sn, sn], axis=1))


def make_in_maps(x_prompt, x_sample, w_in, conv_w, q_norm_g, w_uq, kv_norm_g, w_ukv, w_o,
                 ln1_g, ln1_b, peer_wq, peer_k1, peer_k2, peer_u, peer_v, ln2_g, ln2_b):
    f = lambda a: np.ascontiguousarray(np.asarray(a, dtype=np.float32))
    x_prompt, x_sample = f(x_prompt), f(x_sample)
    shared = {
        "ident": np.eye(128, dtype=np.float32),
        "w_in": f(w_in[0]), "conv_wT": f(np.asarray(conv_w[0]).T), "q_norm_g": f(q_norm_g), "w_uq": f(w_uq[0]),
        "kv_norm_g": f(kv_norm_g), "w_ukv": f(w_ukv[0]), "w_o": f(w_o[0]), "ln1_g": f(ln1_g), "ln1_b": f(ln1_b),
        "peer_wq": f(peer_wq[0]),
        "k1T": f(np.transpose(np.asarray(peer_k1[0]), (2, 0, 1))), "k2T": f(np.transpose(np.asarray(peer_k2[0]), (2, 0, 1))),
        "uT": f(np.asarray(peer_u[0]).reshape(128, 128, 1024).transpose(1, 0, 2).reshape(16384, 1024).T),
        "v": f(np.asarray(peer_v[0]).reshape(128, 128, 1024).transpose(1, 0, 2).reshape(16384, 1024)),
        "ln2_g": f(ln2_g), "ln2_b": f(ln2_b),
        "rope_kv": _rope_table(np.arange(16384)),
    }
    xTkv = [f(x_prompt[p].T) for p in range(2)]
    maps = []
    for c in range(NCORES):
        p, qr = c // 4, c % 4
        own = np.concatenate([x_prompt[p, qr * 4096:(qr + 1) * 4096], x_sample[c]], axis=0)
        halo = np.zeros((32, 1024), np.float32)
        for b in range(16):
            job, jb = b // 8, b % 8
            seq = x_prompt[p] if job == 0 else x_sample[c]
            start = (qr * 4096 if job == 0 else 0) + jb * 512
            if start - 1 >= 0:
                halo[2 * b] = seq[start - 1]
            if start + 512 < seq.shape[0]:
                halo[2 * b + 1] = seq[start + 512]
        pos = np.concatenate([np.arange(qr * 4096, (qr + 1) * 4096), np.arange(4096)])
        m = dict(shared)
        m.update({"xTq": f(own.T), "xTh": f(halo.T), "xTkv": xTkv[p], "xtok": f(own), "rope_q": _rope_table(pos)})
        maps.append(m)
    return maps


def kernel(**inputs):
    maps = make_in_maps(**inputs)
    nc = build_program()
    res = run_bass_kernel_spmd(nc, maps, core_ids=list(range(NCORES)))
    y_prompt = np.zeros((2, 16384, 1024), np.float32)
    y_sample = np.zeros((8, 4096, 1024), np.float32)
    for c in range(NCORES):
        y = np.asarray(res.results[c]["y"], dtype=np.float32)
        p, qr = c // 4, c % 4
        y_prompt[p, qr * 4096:(qr + 1) * 4096] = y[0:4096]
        y_sample[c] = y[4096:8192]
    return (y_prompt, y_sample)
```

```python
import contextlib
import numpy as np
import concourse.bass as bass
import concourse.mybir as mybir
from concourse.bass_utils import run_bass_kernel_spmd

F32 = mybir.dt.float32
BF16 = mybir.dt.bfloat16
AF = mybir.ActivationFunctionType
ALU = mybir.AluOpType
AX = mybir.AxisListType

ALPHA = 2.0 ** 0.25
LN_EPS = 1e-5
RMS_EPS = 1e-6
NCORES = 8
NQ = 8192
NKV = 20480
DEBUG = set()
import os as _os
SKIP6 = set(_os.environ.get('SKIP6', ''))


def mkap(base, dims, off=0):
    return bass.AP(tensor=base.tensor, offset=base.offset + off,
                   ap=[list(base.ap[0])] + [list(d) for d in dims])


class Prog:
    uid = 0

    def __init__(self, nc):
        self.nc = nc
        self.ops = []
        self.last_w = {}
        self.readers = {}

    def op(self, eng, fn, r=(), w=(), sem=None):
        idx = len(self.ops)
        deps = set()
        for k in r:
            if k in self.last_w:
                deps.add(self.last_w[k])
        for k in w:
            if k in self.last_w:
                deps.add(self.last_w[k])
            for rd in self.readers.get(k, ()):
                deps.add(rd)
        o = dict(eng=eng, fn=fn, deps=deps, dma=sem is not None, sem=sem, signaled=False, tok=None)
        self.ops.append(o)
        for k in w:
            self.last_w[k] = idx
            self.readers[k] = []
        for k in r:
            self.readers.setdefault(k, []).append(idx)
        return idx

    def dma(self, out, in_, r=(), w=(), sem=None, q="sp"):
        return self.op(q, lambda e: e.dma_start(out=out, in_=in_), r=r, w=w, sem=sem)

    def emit(self, final_waits=()):
        nc = self.nc
        ops = self.ops
        for o in ops:
            nd = set()
            for d in o["deps"]:
                p = ops[d]
                if p["eng"] == "pe" and o["eng"] == "pe":
                    continue
                nd.add(d)
                p["signaled"] = True
            o["deps"] = nd
        es = contextlib.ExitStack()
        sems = {}
        cnt = {}
        for o in ops:
            if o["dma"]:
                key = "d_" + o["sem"]
                cnt[key] = cnt.get(key, 0) + 16
                o["tok"] = (key, cnt[key])
            elif o["signaled"]:
                key = "c_" + o["eng"]
                cnt[key] = cnt.get(key, 0) + 1
                o["tok"] = (key, cnt[key])
        Prog.uid += 1
        for k in cnt:
            sems[k] = es.enter_context(nc.semaphore(f"sem{Prog.uid}_{k}"))
        per_eng = {}
        for i, o in enumerate(ops):
            per_eng.setdefault(o["eng"], []).append(i)
        fin = {}
        for d in final_waits:
            s, v = ops[d]["tok"]
            fin[s] = max(fin.get(s, 0), v)
        block = es.enter_context(nc.Block())
        engmap = {"pe": "tensor", "act": "scalar", "dve": "vector", "pool": "gpsimd", "sp": "sync"}

        def make_body(idxs):
            def body(e):
                waited = {}
                for i in idxs:
                    o = ops[i]
                    need = {}
                    for d in o["deps"]:
                        s, v = ops[d]["tok"]
                        if v > need.get(s, 0):
                            need[s] = v
                    for s, v in need.items():
                        if waited.get(s, 0) < v:
                            e.wait_ge(sems[s], v)
                            waited[s] = v
                    ins = o["fn"](e)
                    if o["tok"] is not None:
                        ins.then_inc(sems[o["tok"][0]], 16 if o["dma"] else 1)
                for s, v in fin.items():
                    e.wait_ge(sems[s], v)
            return body
        for engname in ("sp", "pe", "act", "dve", "pool"):
            getattr(block, engmap[engname])(make_body(per_eng.get(engname, [])))
        es.close()


class Ctx:
    pass


def declare(nc):
    c = Ctx()

    def din(name, shape, dt=F32):
        return nc.dram_tensor(name, list(shape), dt, kind="ExternalInput").ap()

    def dsc(name, shape, dt=BF16):
        return nc.dram_tensor(name, list(shape), dt, kind="Internal").ap()
    c.xTq = din("xTq", [1024, NQ])
    c.xTh = din("xTh", [1024, 32])
    c.xTkv = din("xTkv", [1024, 16384])
    c.xtok = din("xtok", [NQ, 1024])
    c.rope_kv = din("rope_kv", [16384, 64])
    c.rope_q = din("rope_q", [NQ, 64])
    c.ident = din("ident", [128, 128])
    c.w_in = din("w_in", [1024, 1952])
    c.conv_wT = din("conv_wT", [512, 3])
    c.q_norm_g = din("q_norm_g", [1, 256])
    c.w_uq = din("w_uq", [256, 768])
    c.kv_norm_g = din("kv_norm_g", [1, 128])
    c.w_ukv = din("w_ukv", [128, 1024])
    c.w_o = din("w_o", [1024, 1024])
    c.ln1_g = din("ln1_g", [1, 1024])
    c.ln1_b = din("ln1_b", [1, 1024])
    c.peer_wq = din("peer_wq", [1024, 2048])
    c.k1T = din("k1T", [128, 8, 128])
    c.k2T = din("k2T", [128, 8, 128])
    c.uT = din("uT", [1024, 16384])
    c.v = din("v", [16384, 1024])
    c.ln2_g = din("ln2_g", [1, 1024])
    c.ln2_b = din("ln2_b", [1, 1024])
    c.y = nc.dram_tensor("y", [NQ, 1024], F32, kind="ExternalOutput").ap()
    c.uT_bf = dsc("uT_bf", [1024, 16384])
    c.v_bf = dsc("v_bf", [16384, 1024])
    c.wq_bf = dsc("wq_bf", [1024, 2048])
    c.wo_bf = dsc("wo_bf", [1024, 1024])
    c.x1_s = dsc("x1_s", [NQ, 1024], F32)
    c.KT_s = dsc("KT_s", [8, 96, NKV])
    c.V_s = dsc("V_s", [8, 128, NKV // 128, 128])
    c.QT_s = dsc("QT_s", [8, 96, NQ])
    c.mixT_s = dsc("mixT_s", [1024, NQ])
    return c


class Phase:
    def __init__(self, nc):
        self.nc = nc
        self.es = contextlib.ExitStack()
        self.P = Prog(nc)
        self.stores = []

    def sb(self, name, shape, dt=F32):
        return self.es.enter_context(self.nc.sbuf_tensor(name, list(shape), dt))

    def ps(self, name, shape, dt=F32):
        return self.es.enter_context(self.nc.psum_tensor(name, list(shape), dt))

    def store(self, out, in_, r, sem):
        i = self.P.dma(out, in_, r=r, sem=sem)
        self.stores.append(i)
        return i

    def finish(self):
        self.P.emit(final_waits=self.stores)
        self.es.close()


CAST_ENG = ("dve", "act", "pool")


def cast(P, eng, out, in_, r, w):
    if eng == "act":
        return P.op("act", lambda e: e.activation(out=out, in_=in_, func=AF.Copy), r=r, w=w)
    return P.op(eng, lambda e: e.tensor_copy(out=out, in_=in_), r=r, w=w)


def load_weight_bf16(ph, name, dram_view, shape, stage, stage_key, eng="dve"):
    P = ph.P
    wt = ph.sb(name, shape, BF16)
    sview = stage
    P.dma(sview, dram_view, w=[stage_key], sem="ld_" + stage_key)
    cast(P, eng, wt[:], sview, r=[stage_key], w=[name])
    return wt


def w_jobs(ph, c, q="sp", engs=CAST_ENG):
    P = ph.P
    st = [ph.sb(f"w_st{i}", [128, 2048], F32) for i in range(2)]
    ob = [ph.sb(f"w_ob{i}", [128, 2048], BF16) for i in range(2)]
    uT_v = c.uT.rearrange("(a p) n -> p a n", p=128)
    uTb_v = c.uT_bf.rearrange("(a p) n -> p a n", p=128)
    v_v = c.v.rearrange("(a p) d -> p a d", p=128)
    vb_v = c.v_bf.rearrange("(a p) d -> p a d", p=128)
    jobs = []
    for a in range(8):
        for cb in range(8):
            jobs.append((uT_v[:, a, cb * 2048:(cb + 1) * 2048], uTb_v[:, a, cb * 2048:(cb + 1) * 2048], None))
    for a2 in range(64):
        jobs.append((v_v[:, 2 * a2:2 * a2 + 2, :], vb_v[:, 2 * a2:2 * a2 + 2, :], 2))
    wq_v = c.peer_wq.rearrange("(a p) n -> p a n", p=128)
    wqb_v = c.wq_bf.rearrange("(a p) n -> p a n", p=128)
    for a in range(8):
        jobs.append((wq_v[:, a, :], wqb_v[:, a, :], None))
    wo_v = c.w_o.rearrange("(a p) n -> p a n", p=128)
    wob_v = c.wo_bf.rearrange("(a p) n -> p a n", p=128)
    for a2 in range(4):
        jobs.append((wo_v[:, 2 * a2:2 * a2 + 2, :], wob_v[:, 2 * a2:2 * a2 + 2, :], 2))
    out = []
    for it, (src, dst, three) in enumerate(jobs):
        def rec(it=it, src=src, dst=dst, three=three):
            s = it % 2
            sv = st[s][:] if three is None else st[s][:].rearrange("p (a d) -> p a d", a=2)
            ov = ob[s][:] if three is None else ob[s][:].rearrange("p (a d) -> p a d", a=2)
            P.dma(sv, src, w=[f"wst{s}"], sem=f"wst{s}", q=q)
            cast(P, engs[it % len(engs)], ov, sv, r=[f"wst{s}"], w=[f"wob{s}"])
            ph.stores.append(P.dma(dst, ov, r=[f"wob{s}"], sem=f"wob{s}", q=q))
        out.append(rec)
    return out


def phase_W(nc, c):
    ph = Phase(nc)
    for rec in w_jobs(ph, c):
        rec()
    ph.finish()


def rmsnorm_rows(ph, pfx, zsrc, width, gam, out_bf, rkeys, wkey, slot):
    P = ph.P
    ss, sd, rs, junk = ph.t[pfx + "ss"][slot], ph.t[pfx + "sd"][slot], ph.t[pfx + "rs"][slot], ph.t[pfx + "junk"][slot]
    k = f"{pfx}{slot}"
    P.op("pool", lambda e: e.memset(ss[:], 0.0), w=[k + "ss"])
    P.op("act", lambda e: e.activation(out=junk[:, 0:width], in_=zsrc, func=AF.Square, accum_out=ss[:, 0:1]),
         r=rkeys + [k + "ss"], w=[k + "ss", k + "junk"] + rkeys)
    P.op("act", lambda e: e.activation(out=sd[:], in_=ss[:], func=AF.Sqrt, bias=RMS_EPS, scale=1.0 / width),
         r=[k + "ss"], w=[k + "sd"])
    P.op("dve", lambda e: e.reciprocal(out=rs[:], in_=sd[:]), r=[k + "sd"], w=[k + "rs"])
    P.op("dve", lambda e: e.scalar_tensor_tensor(out=out_bf, in0=zsrc, scalar=rs[:, 0:1], in1=gam,
                                                 op0=ALU.mult, op1=ALU.mult),
         r=rkeys + [k + "rs", "gam"], w=[wkey] + rkeys)


def phase_A(nc, c, nkv=NKV // 128, nq=NQ // 128, upto=99):
    ph = Phase(nc)
    P = ph.P
    sb, ps = ph.sb, ph.ps
    ph.t = {}
    stage = sb("a_stage", [128, 8, 256], F32)
    idf = sb("a_idf", [128, 128], F32)
    idb = sb("a_idb", [128, 128], BF16)
    P.dma(idf[:], c.ident[:, :], w=["idf"], sem="idf")
    cast(P, "dve", idb[:], idf[:], r=["idf"], w=["idb"])
    w_in_v = c.w_in.rearrange("(kc p) n -> p kc n", p=128)
    wkv = load_weight_bf16(ph, "wkv", w_in_v[:, :, 1792:1952], [128, 8, 160], stage[:, :, 0:160], "stage", "dve")
    wq_ = load_weight_bf16(ph, "wq", w_in_v[:, :, 1536:1792], [128, 8, 256], stage[:, :, 0:256], "stage", "act")
    wukv = load_weight_bf16(ph, "wukv", c.w_ukv[:, :], [128, 1024], stage[:].rearrange("p a b -> p (a b)")[:, 0:1024], "stage", "dve")
    wuq = load_weight_bf16(ph, "wuq", c.w_uq.rearrange("(kc p) n -> p kc n", p=128), [128, 2, 768],
                           stage[:].rearrange("p a b -> p (a b)")[:, 0:1536].rearrange("p (k n) -> p k n", k=2), "stage", "act")
    gkv = sb("a_gkv", [128, 128], F32)
    gq = sb("a_gq", [128, 256], F32)
    P.dma(gkv[:], c.kv_norm_g[0:1, :].partition_broadcast(128), w=["gam"], sem="gam")
    P.dma(gq[:], c.q_norm_g[0:1, :].partition_broadcast(128), w=["gam"], sem="gam")

    NS = 2
    xs = [sb(f"a_xs{i}", [128, 8, 128], F32) for i in range(3)]
    xb = [sb(f"a_xb{i}", [128, 8, 128], BF16) for i in range(3)]
    rp = [sb(f"a_rp{i}", [128, 64], F32) for i in range(3)]
    for pfx, wd in (("kv", 128), ("q", 256)):
        ph.t[pfx + "ss"] = [sb(f"a_{pfx}ss{i}", [128, 1], F32) for i in range(NS)]
        ph.t[pfx + "sd"] = [sb(f"a_{pfx}sd{i}", [128, 1], F32) for i in range(NS)]
        ph.t[pfx + "rs"] = [sb(f"a_{pfx}rs{i}", [128, 1], F32) for i in range(NS)]
        ph.t[pfx + "junk"] = [sb(f"a_{pfx}jk{i}", [128, 256], F32) for i in range(NS)]
    kvn = [sb(f"a_kvn{i}", [128, 128], BF16) for i in range(NS)]
    kvnT = [sb(f"a_kvnT{i}", [128, 128], BF16) for i in range(NS)]
    tA = [sb(f"a_tA{i}", [128, 8, 32], F32) for i in range(NS)]
    tB = [sb(f"a_tB{i}", [128, 8, 32], F32) for i in range(NS)]
    kr = [sb(f"a_kr{i}", [128, 32], F32) for i in range(NS)]
    Ktok = [sb(f"a_Ktok{i}", [128, 8, 96], BF16) for i in range(NS)]
    Vtok = [sb(f"a_Vtok{i}", [128, 8, 128], BF16) for i in range(NS)]
    KTs = [sb(f"a_KTs{i}", [96, 8, 128], BF16) for i in range(NS)]
    qn = [sb(f"a_qn{i}", [128, 256], BF16) for i in range(NS)]
    qnT = [sb(f"a_qnT{i}", [128, 2, 128], BF16) for i in range(NS)]
    zp = [ps(f"a_zp{i}", [128, 512], F32) for i in range(2)]
    trp = ps("a_trp", [128, 1024], BF16)
    kvp = ps("a_kvp", [128, 1024], F32)
    KTp = [ps(f"a_KTp{i}", [128, 1024], BF16) for i in range(2)]
    for i in range(NS):
        P.op("pool", lambda e, i=i: e.memset(Vtok[i][:], 0.0), w=[f"Vtok{i}"])
        P.op("pool", lambda e, i=i: e.memset(Vtok[i][:, :, 0:1], 1.0), w=[f"Vtok{i}"])

    xTkv_v = c.xTkv.rearrange("(kc p) t -> p kc t", p=128)
    xTq_v = c.xTq.rearrange("(kc p) t -> p kc t", p=128)
    KT_v = c.KT_s.rearrange("h r t -> r h t")
    V_v = c.V_s.rearrange("h p c e -> p h c e")
    QT_v = c.QT_s.rearrange("h r t -> r h t")

    def load_x(it, src_view, t0, rope_src, r0):
        s3 = it % 3
        P.dma(xs[s3][:], src_view[:, :, t0:t0 + 128], w=[f"xs{s3}"], sem=f"xs{s3}")
        P.dma(rp[s3][:], rope_src[r0:r0 + 128, :], w=[f"rp{s3}"], sem=f"rp{s3}")
        cast(P, "act" if it % 2 == 0 else "dve", xb[s3][:], xs[s3][:], r=[f"xs{s3}"], w=[f"xb{s3}"])

    def kv_load(it):
        if it < 128:
            load_x(it, xTkv_v, it * 128, c.rope_kv, it * 128)
        else:
            load_x(it, xTq_v, 4096 + (it - 128) * 128, c.rope_kv, (it - 128) * 128)

    def kv_s1(it):
        s, s3 = it % NS, it % 3
        z = zp[s]
        for kc in range(8):
            P.op("pe", lambda e, kc=kc: e.matmul(z[:, 0:160], xb[s3][:, kc, :], wkv[:, kc, :], start=(kc == 0), stop=(kc == 7)),
                 r=[f"xb{s3}", "wkv"], w=[f"zp{s}"])
        rmsnorm_rows(ph, "kv", z[:, 0:128], 128, gkv[:], kvn[s][:], [f"zp{s}"], f"kvn{s}", s)
        P.op("dve", lambda e: e.tensor_tensor(out=tA[s][:, 0, :], in0=z[:, 128:160], in1=rp[s3][:, 0:32], op=ALU.mult),
             r=[f"zp{s}", f"rp{s3}"], w=[f"tA{s}0", f"tA{s}1", f"zp{s}"])
        P.op("dve", lambda e: e.tensor_tensor(out=tB[s][:, 0, 0:16], in0=z[:, 144:160], in1=rp[s3][:, 32:48], op=ALU.mult),
             r=[f"zp{s}", f"rp{s3}"], w=[f"tB{s}a0", f"tB{s}a1", f"zp{s}"])
        P.op("dve", lambda e: e.tensor_tensor(out=tB[s][:, 0, 16:32], in0=z[:, 128:144], in1=rp[s3][:, 48:64], op=ALU.mult),
             r=[f"zp{s}", f"rp{s3}"], w=[f"tB{s}b0", f"tB{s}b1", f"zp{s}"])
        P.op("pool", lambda e: e.tensor_tensor(out=kr[s][:], in0=tA[s][:, 0, :], in1=tB[s][:, 0, :], op=ALU.add),
             r=[f"tA{s}0", f"tB{s}a0", f"tB{s}b0"], w=[f"kr{s}"])

    def kv_s2(it):
        s = it % NS
        P.op("pe", lambda e: e.transpose(out=trp[:, 0:128], in_=kvn[s][:], identity=idb[:]), r=[f"kvn{s}", "idb"], w=["trp"])
        P.op("act", lambda e: e.activation(out=kvnT[s][:], in_=trp[:, 0:128], func=AF.Copy), r=["trp"], w=[f"kvnT{s}", "trp"])
        for half in range(2):
            P.op("pe", lambda e, half=half: e.matmul(kvp[:, half * 512:(half + 1) * 512], kvnT[s][:], wukv[:, half * 512:(half + 1) * 512], start=True, stop=True),
                 r=[f"kvnT{s}", "wukv"], w=[f"kvp{half}"])
        for half in range(2):
            kv4 = kvp[:, half * 512:(half + 1) * 512].rearrange("p (h e) -> p h e", e=128)
            P.op("act", lambda e, kv4=kv4, half=half: e.activation(out=Ktok[s][:, half * 4:(half + 1) * 4, 32:96], in_=kv4[:, :, 0:64], func=AF.Copy),
                 r=[f"kvp{half}"], w=[f"Ktok{s}n{half}", f"kvp{half}"])
            P.op("dve", lambda e, kv4=kv4, half=half: e.tensor_copy(out=Vtok[s][:, half * 4:(half + 1) * 4, 64:128], in_=kv4[:, :, 64:128]),
                 r=[f"kvp{half}"], w=[f"Vtok{s}", f"kvp{half}"])
        P.op("pool", lambda e: e.tensor_copy(out=Ktok[s][:, :, 0:32], in_=mkap(kr[s][:], [[0, 8], [1, 32]])), r=[f"kr{s}"], w=[f"Ktok{s}r"])
        ktp = KTp[s]
        for h in range(8):
            P.op("pe", lambda e, h=h: e.transpose(out=ktp[0:96, h * 128:(h + 1) * 128], in_=Ktok[s][:, h, :], identity=idb[:]),
                 r=[f"Ktok{s}n0", f"Ktok{s}n1", f"Ktok{s}r", "idb"], w=[f"KTp{s}"])
        if it % 2 == 0:
            P.op("dve", lambda e: e.tensor_copy(out=KTs[s][:], in_=ktp[0:96, :].rearrange("p (h t) -> p h t", h=8)), r=[f"KTp{s}"], w=[f"KTs{s}", f"KTp{s}"])
        else:
            P.op("act", lambda e: e.activation(out=KTs[s][:], in_=ktp[0:96, :].rearrange("p (h t) -> p h t", h=8), func=AF.Copy), r=[f"KTp{s}"], w=[f"KTs{s}", f"KTp{s}"])
        ph.store(KT_v[:, :, it * 128:(it + 1) * 128], KTs[s][:], r=[f"KTs{s}"], sem=f"KTs{s}")
        ph.store(V_v[:, :, it, :], Vtok[s][:], r=[f"Vtok{s}"], sem=f"Vtok{s}")

    for it in range(min(2, nkv)):
        kv_load(it)
    if nkv > 0:
        kv_s1(0)
    for it in range(nkv):
        if it + 2 < nkv:
            kv_load(it + 2)
        if it + 1 < nkv:
            kv_s1(it + 1)
        kv_s2(it)

    qp = kvp

    def q_load(it):
        load_x(NKV // 128 + it, xTq_v, it * 128, c.rope_q, it * 128)

    def q_s1(it):
        s, s3 = it % NS, (NKV // 128 + it) % 3
        z = zp[s]
        for kc in range(8):
            P.op("pe", lambda e, kc=kc: e.matmul(z[:, 0:256], xb[s3][:, kc, :], wq_[:, kc, :], start=(kc == 0), stop=(kc == 7)),
                 r=[f"xb{s3}", "wq"], w=[f"zp{s}"])
        rmsnorm_rows(ph, "q", z[:, 0:256], 256, gq[:], qn[s][:], [f"zp{s}"], f"qn{s}", s)

    def q_s2(it):
        s, s3 = it % NS, (NKV // 128 + it) % 3
        for k2 in range(2):
            P.op("pe", lambda e, k2=k2: e.transpose(out=trp[:, k2 * 128:(k2 + 1) * 128], in_=qn[s][:, k2 * 128:(k2 + 1) * 128], identity=idb[:]),
                 r=[f"qn{s}", "idb"], w=["trp"])
        P.op("act", lambda e: e.activation(out=qnT[s][:], in_=trp[:, 0:256].rearrange("p (k t) -> p k t", k=2), func=AF.Copy),
             r=["trp"], w=[f"qnT{s}", "trp"])
        for (c0, c1, o0, key) in ((0, 480, 0, "kvp0"), (480, 768, 512, "kvp1")):
            for k2 in range(2):
                P.op("pe", lambda e, k2=k2, c0=c0, c1=c1, o0=o0: e.matmul(qp[:, o0:o0 + (c1 - c0)], qnT[s][:, k2, :], wuq[:, k2, c0:c1],
                                                                       start=(k2 == 0), stop=(k2 == 1)),
                     r=[f"qnT{s}", "wuq"], w=[key])
        for (h0, nh, o0, key, half) in ((0, 5, 0, "kvp0", 0), (5, 3, 512, "kvp1", 1)):
            q3 = qp[:, o0:o0 + nh * 96].rearrange("p (h e) -> p h e", e=96)
            P.op("act", lambda e, q3=q3, h0=h0, nh=nh: e.activation(out=Ktok[s][:, h0:h0 + nh, 32:96], in_=q3[:, :, 0:64], func=AF.Copy),
                 r=[key], w=[f"Ktok{s}n{half}", key])
            cc_b = mkap(rp[s3][:, 0:32], [[0, nh], [1, 32]])
            ns_b = mkap(rp[s3][:, 32:48], [[0, nh], [1, 16]])
            sn_b = mkap(rp[s3][:, 48:64], [[0, nh], [1, 16]])
            P.op("dve", lambda e, cc_b=cc_b, q3=q3, h0=h0, nh=nh: e.tensor_tensor(out=tA[s][:, h0:h0 + nh, :], in0=q3[:, :, 64:96], in1=cc_b, op=ALU.mult),
                 r=[key, f"rp{s3}"], w=[f"tA{s}{half}", key])
            P.op("dve", lambda e, ns_b=ns_b, q3=q3, h0=h0, nh=nh: e.tensor_tensor(out=tB[s][:, h0:h0 + nh, 0:16], in0=q3[:, :, 80:96], in1=ns_b, op=ALU.mult),
                 r=[key, f"rp{s3}"], w=[f"tB{s}a{half}", key])
            P.op("dve", lambda e, sn_b=sn_b, q3=q3, h0=h0, nh=nh: e.tensor_tensor(out=tB[s][:, h0:h0 + nh, 16:32], in0=q3[:, :, 64:80], in1=sn_b, op=ALU.mult),
                 r=[key, f"rp{s3}"], w=[f"tB{s}b{half}", key])
        P.op("pool", lambda e: e.tensor_tensor(out=Ktok[s][:, :, 0:32], in0=tA[s][:], in1=tB[s][:], op=ALU.add),
             r=[f"tA{s}0", f"tA{s}1", f"tB{s}a0", f"tB{s}a1", f"tB{s}b0", f"tB{s}b1"], w=[f"Ktok{s}r"])
        ktp = KTp[s]
        for h in range(8):
            P.op("pe", lambda e, h=h: e.transpose(out=ktp[0:96, h * 128:(h + 1) * 128], in_=Ktok[s][:, h, :], identity=idb[:]),
                 r=[f"Ktok{s}n0", f"Ktok{s}n1", f"Ktok{s}r", "idb"], w=[f"KTp{s}"])
        P.op("dve", lambda e: e.tensor_copy(out=KTs[s][:], in_=ktp[0:96, :].rearrange("p (h t) -> p h t", h=8)), r=[f"KTp{s}"], w=[f"KTs{s}", f"KTp{s}"])
        ph.store(QT_v[:, :, it * 128:(it + 1) * 128], KTs[s][:], r=[f"KTs{s}"], sem=f"KTs{s}")

    for it in range(min(2, nq)):
        q_load(it)
    if nq > 0:
        q_s1(0)
    for it in range(nq):
        if it + 2 < nq:
            q_load(it + 2)
        if it + 1 < nq:
            q_s1(it + 1)
        q_s2(it)
    ph.finish()


def phase_A3(nc, c):
    ph = Phase(nc)
    P = ph.P
    sb, ps = ph.sb, ph.ps
    stage = sb("c_stage", [128, 8, 1536], F32)
    w_in_v = c.w_in.rearrange("(kc p) n -> p kc n", p=128)
    wc = load_weight_bf16(ph, "wc", w_in_v[:, :, 0:1536], [128, 8, 1536], stage[:], "stage", "pool")
    cw = sb("c_cw", [128, 4, 3], F32)
    P.dma(cw[:], c.conv_wT.rearrange("(cc p) k -> p cc k", p=128), w=["cw"], sem="cw")
    xh = sb("c_xh", [128, 8, 32], F32)
    xhb = sb("c_xhb", [128, 8, 32], BF16)
    uh = sb("c_uh", [128, 4, 32], F32)
    chs = sb("c_chs", [128, 32], F32)
    P.dma(xh[:], c.xTh.rearrange("(kc p) t -> p kc t", p=128), w=["xh"], sem="xh")
    cast(P, "dve", xhb[:], xh[:], r=["xh"], w=["xhb"])
    pb = [ps(f"c_pb{i}", [128, 512], F32) for i in range(2)]
    pc = [ps(f"c_pc{i}", [128, 512], F32) for i in range(2)]
    phh = [ps(f"c_ph{i}", [128, 512], F32) for i in range(2)]
    for cc in range(4):
        for (dst, col, key) in ((pc[0], 512, "pc0"), (phh[0], 1024, "ph0")):
            for kc in range(8):
                P.op("pe", lambda e, dst=dst, col=col, kc=kc, cc=cc: e.matmul(dst[:, 0:32], wc[:, kc, col + cc * 128: col + (cc + 1) * 128],
                                                                              xhb[:, kc, :], start=(kc == 0), stop=(kc == 7)),
                     r=["wc", "xhb"], w=[key])
        P.op("act", lambda e: e.activation(out=chs[:], in_=pc[0][:, 0:32], func=AF.Copy), r=["pc0"], w=["chs"])
        P.op("dve", lambda e, cc=cc: e.tensor_tensor(out=uh[:, cc, :], in0=chs[:], in1=phh[0][:, 0:32], op=ALU.mult),
             r=["chs", "ph0"], w=["uh"])
    xs = [sb(f"c_xs{i}", [128, 8, 512], F32) for i in range(2)]
    xb = [sb(f"c_xb{i}", [128, 8, 512], BF16) for i in range(2)]
    csb = [sb(f"c_csb{i}", [128, 512], F32) for i in range(2)]
    ub = [sb(f"c_ub{i}", [128, 514], F32) for i in range(2)]
    yb = [sb(f"c_yb{i}", [128, 512], F32) for i in range(2)]
    cv = [sb(f"c_cv{i}", [128, 512], BF16) for i in range(2)]
    xTq_v = c.xTq.rearrange("(kc p) t -> p kc t", p=128)
    j = 0
    for b in range(NQ // 512):
        s = b % 2
        for hh in range(2):
            P.dma(xs[s][:, hh * 4:(hh + 1) * 4, :], xTq_v[:, hh * 4:(hh + 1) * 4, b * 512:(b + 1) * 512], w=[f"xs{s}{hh}"], sem=f"xs{s}{hh}")
            cast(P, "act" if hh == 0 else "dve", xb[s][:, hh * 4:(hh + 1) * 4, :], xs[s][:, hh * 4:(hh + 1) * 4, :],
                 r=[f"xs{s}{hh}"], w=[f"xb{s}{hh}"])
        for cc in range(4):
            t = j % 2
            j += 1
            for (dst, col, key) in ((pb[t], 0, f"pb{t}"), (pc[t], 512, f"pc{t}"), (phh[t], 1024, f"ph{t}")):
                for kc in range(8):
                    P.op("pe", lambda e, dst=dst, col=col, kc=kc, cc=cc, s=s: e.matmul(dst[:], wc[:, kc, col + cc * 128: col + (cc + 1) * 128],
                                                                                       xb[s][:, kc, :], start=(kc == 0), stop=(kc == 7)),
                         r=["wc", f"xb{s}0", f"xb{s}1"], w=[key])
            P.op("act", lambda e, t=t: e.activation(out=csb[t][:], in_=pc[t][:], func=AF.Copy), r=[f"pc{t}"], w=[f"csb{t}"])
            P.op("dve", lambda e, t=t: e.tensor_tensor(out=ub[t][:, 1:513], in0=csb[t][:], in1=phh[t][:], op=ALU.mult),
                 r=[f"csb{t}", f"ph{t}"], w=[f"ub{t}m"])
            P.op("pool", lambda e, t=t, cc=cc, b=b: e.tensor_copy(out=ub[t][:, 0:1], in_=uh[:, cc, 2 * b:2 * b + 1]), r=["uh"], w=[f"ub{t}l"])
            P.op("pool", lambda e, t=t, cc=cc, b=b: e.tensor_copy(out=ub[t][:, 513:514], in_=uh[:, cc, 2 * b + 1:2 * b + 2]), r=["uh"], w=[f"ub{t}r"])
            ubk = [f"ub{t}m", f"ub{t}l", f"ub{t}r"]
            P.op("dve", lambda e, t=t, cc=cc: e.tensor_scalar(out=yb[t][:], in0=ub[t][:, 0:512], scalar1=cw[:, cc, 0:1], scalar2=None, op0=ALU.mult),
                 r=ubk + ["cw"], w=[f"yb{t}"])
            P.op("dve", lambda e, t=t, cc=cc: e.scalar_tensor_tensor(out=yb[t][:], in0=ub[t][:, 1:513], scalar=cw[:, cc, 1:2], in1=yb[t][:],
                                                                      op0=ALU.mult, op1=ALU.add),
                 r=ubk + ["cw", f"yb{t}"], w=[f"yb{t}"])
            P.op("dve", lambda e, t=t, cc=cc: e.scalar_tensor_tensor(out=yb[t][:], in0=ub[t][:, 2:514], scalar=cw[:, cc, 2:3], in1=yb[t][:],
                                                                      op0=ALU.mult, op1=ALU.add),
                 r=ubk + ["cw", f"yb{t}"], w=[f"yb{t}"])
            P.op("dve", lambda e, t=t: e.tensor_tensor(out=cv[t][:], in0=yb[t][:], in1=pb[t][:], op=ALU.mult),
                 r=[f"yb{t}", f"pb{t}"], w=[f"cv{t}"])
            ph.store(c.mixT_s[cc * 128:(cc + 1) * 128, b * 512:(b + 1) * 512], cv[t][:], r=[f"cv{t}"], sem=f"cv{t}")
    ph.finish()


def phase_B(nc, c, with_w=True):
    ph = Phase(nc)
    P = ph.P
    sb, ps = ph.sb, ph.ps
    scale = 96.0 ** -0.5
    KT = [sb(f"b_KT{i}", [96, 16384], BF16) for i in range(2)]
    Vh = [sb(f"b_V{i}", [128, 128, 128], BF16) for i in range(2)]
    QT = [sb(f"b_QT{i}", [96, 512], BF16) for i in range(2)]
    pt = [sb(f"b_pt{i}", [128, 512], BF16) for i in range(4)]
    rd = sb("b_rd", [1, 512], F32)
    ones = sb("b_ones", [1, 128], F32)
    bcs = sb("b_bcs", [128, 512], F32)
    at = [sb(f"b_at{i}", [128, 512], BF16) for i in range(2)]
    sp_ = [ps(f"b_sp{i}", [128, 512], F32) for i in range(4)]
    op_ = [ps(f"b_op{i}", [128, 512], F32) for i in range(2)]
    bcp = ps("b_bcp", [128, 512], F32)
    P.op("pool", lambda e: e.memset(ones[:], 1.0), w=["ones"])
    heads = [(job, h) for job in range(2) for h in range(8)]

    def load_kv(hi):
        job, h = heads[hi]
        s = hi % 2
        nk = 16384 if job == 0 else 4096
        koff = 0 if job == 0 else 16384
        nq4 = nk // 4
        for qq in range(4):
            P.dma(KT[s][:, qq * nq4:(qq + 1) * nq4], c.KT_s[h, :, koff + qq * nq4: koff + (qq + 1) * nq4], w=[f"KT{s}"], sem=f"KT{s}")
        nkc = nk // 128
        P.dma(Vh[s][:, 0:nkc, :], c.V_s[h, :, koff // 128: koff // 128 + nkc, :], w=[f"V{s}"], sem=f"V{s}")

    NSP = 4
    LOOK = 3
    load_kv(0)
    jobs_q = [(hi, qb) for hi in range(16) for qb in range(8)]

    def load_q(n):
        hi, qb = jobs_q[n]
        job, h = heads[hi]
        qs = n % 2
        q0 = job * 4096 + qb * 512
        P.dma(QT[qs][:], c.QT_s[h, :, q0:q0 + 512], w=[f"QT{qs}"], sem=f"QT{qs}")
    load_q(0)
    units = []
    for n, (hi, qb) in enumerate(jobs_q):
        job, h = heads[hi]
        nkc = 128 if job == 0 else 32
        for kc in range(nkc):
            units.append((n, kc, nkc))
    loaded_q = {0}
    loaded_kv = {0}

    def issue_S(u):
        n, kc, nkc = units[u]
        hi, qb = jobs_q[n]
        if kc == 0 and n + 1 < len(jobs_q) and (n + 1) not in loaded_q:
            load_q(n + 1)
            loaded_q.add(n + 1)
        if kc == 8 and qb == 0 and hi + 1 < 16 and (hi + 1) not in loaded_kv:
            load_kv(hi + 1)
            loaded_kv.add(hi + 1)
        s, qs, k3 = hi % 2, n % 2, u % NSP
        P.op("pe", lambda e: e.matmul(sp_[k3][:], KT[s][:, kc * 128:(kc + 1) * 128], QT[qs][:], start=True, stop=True),
             r=[f"KT{s}", f"QT{qs}"], w=[f"sp{k3}"])
        P.op("act", lambda e: e.activation(out=pt[k3][:], in_=sp_[k3][:], func=AF.Exp, scale=scale),
             r=[f"sp{k3}"], w=[f"pt{k3}", f"sp{k3}"])

    def finalize(n):
        hi, qb = jobs_q[n]
        job, h = heads[hi]
        o = op_[n % 2]
        a = at[n % 2]
        P.op("dve", lambda e: e.reciprocal(out=rd[:], in_=o[0:1, :]), r=[f"op{n % 2}"], w=["rd", f"op{n % 2}"])
        P.op("pe", lambda e: e.matmul(bcp[:, :], ones[:], rd[:], start=True, stop=True), r=["ones", "rd"], w=["bcp"])
        P.op("act", lambda e: e.activation(out=bcs[64:128, :], in_=bcp[64:128, :], func=AF.Copy), r=["bcp"], w=["bcs", "bcp"])
        P.op("dve", lambda e: e.tensor_tensor(out=a[64:128, :], in0=o[64:128, :], in1=bcs[64:128, :], op=ALU.mult),
             r=[f"op{n % 2}", "bcs"], w=[f"at{n % 2}", f"op{n % 2}"])
        q0 = job * 4096 + qb * 512
        ph.store(c.mixT_s[512 + 64 * h: 512 + 64 * (h + 1), q0:q0 + 512], a[64:128, :], r=[f"at{n % 2}"], sem=f"at{n % 2}")

    pending = []
    wj = w_jobs(ph, c, q="pool", engs=("pool", "dve")) if with_w else []
    for u in range(min(LOOK, len(units))):
        issue_S(u)
    for u in range(len(units)):
        if u % 64 == 32 and wj:
            wj.pop(0)()
        if u + LOOK < len(units):
            issue_S(u + LOOK)
        n, kc, nkc = units[u]
        hi, qb = jobs_q[n]
        s, k3 = hi % 2, u % NSP
        o = op_[n % 2]
        P.op("pe", lambda e, s=s, kc=kc, k3=k3, o=o, nkc=nkc: e.matmul(o[:, :], Vh[s][:, kc, :], pt[k3][:], start=(kc == 0), stop=(kc == nkc - 1)),
             r=[f"V{s}", f"pt{k3}"], w=[f"op{n % 2}"])
        if kc == nkc - 1:
            pending.append((u + 6, n))
        while pending and (pending[0][0] <= u or u == len(units) - 1):
            finalize(pending.pop(0)[1])
    while wj:
        wj.pop(0)()
    ph.finish()


def layernorm_rows(ph, pfx, src, g_rep, b_rep, out, rkeys, wkey):
    P = ph.P
    st, mv, sd, rs = ph.t["st"], ph.t["mv"], ph.t["sd"], ph.t["rs"]
    P.op("dve", lambda e: e.bn_stats(out=st[:, 0, :], in_=src[:, 0:512]), r=rkeys, w=["st0"])
    P.op("dve", lambda e: e.bn_stats(out=st[:, 1, :], in_=src[:, 512:1024]), r=rkeys, w=["st1"])
    P.op("dve", lambda e: e.bn_aggr(out=mv[:], in_=st[:].rearrange("p a b -> p (a b)")), r=["st0", "st1"], w=["mv"])
    P.op("act", lambda e: e.activation(out=sd[:], in_=mv[:, 1:2], func=AF.Sqrt, bias=LN_EPS, scale=1.0), r=["mv"], w=["sd"])
    P.op("dve", lambda e: e.reciprocal(out=rs[:], in_=sd[:]), r=["sd"], w=["rs"])
    P.op("dve", lambda e: e.tensor_scalar(out=out, in0=src, scalar1=mv[:, 0:1], scalar2=rs[:, 0:1], op0=ALU.subtract, op1=ALU.mult),
         r=rkeys + ["mv", "rs"], w=[wkey])
    P.op("pool", lambda e: e.tensor_tensor(out=out, in0=out, in1=g_rep, op=ALU.mult), r=[wkey, "lnp"], w=[wkey])
    P.op("pool", lambda e: e.tensor_tensor(out=out, in0=out, in1=b_rep, op=ALU.add), r=[wkey, "lnp"], w=[wkey])


def phase_D(nc, c, ngroups=None):
    TG = 2
    NB = 256
    IB = 8
    ph = Phase(nc)
    P = ph.P
    sb, ps = ph.sb, ph.ps
    ph.t = {}
    stage = sb("d_stage", [128, 1024], F32)
    idf = sb("d_idf", [128, 128], F32)
    idb = sb("d_idb", [128, 128], BF16)
    P.dma(idf[:], c.ident[:, :], w=["idf"], sem="idf")
    cast(P, "dve", idb[:], idf[:], r=["idf"], w=["idb"])
    wo = sb("d_wo", [128, 8, 1024], BF16)
    wqs = [sb(f"d_wqs{i}", [128, 8, 512], BF16) for i in range(2)]
    wo_v = c.w_o.rearrange("(kc p) n -> p kc n", p=128)
    wqb_v = c.wq_bf.rearrange("(kc p) n -> p kc n", p=128)
    for kc in range(8):
        P.dma(stage[:], wo_v[:, kc, :], w=["stage"], sem="stage")
        cast(P, CAST_ENG[kc % 3], wo[:, kc, :], stage[:], r=["stage"], w=["wo"])
    WKEYS = ["wo", "wq"]
    wqn = 0
    kT = [sb(f"d_k{j}T", [128, 8, 128], BF16) for j in range(2)]
    for j, src in enumerate((c.k1T, c.k2T)):
        P.dma(stage[:, 0:1024].rearrange("p (h n) -> p h n", h=8), src[:, :, :], w=["stage"], sem="stage")
        cast(P, "dve", kT[j][:], stage[:, 0:1024].rearrange("p (h n) -> p h n", h=8), r=["stage"], w=[f"kT{j}"])
    lnp = sb("d_lnp", [128, 4, 1024], F32)
    for j, src in enumerate((c.ln1_g, c.ln1_b, c.ln2_g, c.ln2_b)):
        P.dma(lnp[:, j, :], src[0:1, :].partition_broadcast(128), w=["lnp"], sem="lnp")
    ph.t["st"] = sb("d_st", [128, 2, 6], F32)
    ph.t["mv"] = sb("d_mv", [128, 2], F32)
    ph.t["sd"] = sb("d_sd", [128, 1], F32)
    ph.t["rs"] = sb("d_rs", [128, 1], F32)
    mx = sb("d_mx", [128, 8, 128], BF16)
    xt = sb("d_xt", [128, 1024], F32)
    x1 = [sb(f"d_x1_{t}", [128, 1024], F32) for t in range(TG)]
    x1b = sb("d_x1b", [128, 1024], BF16)
    x1T = [sb(f"d_x1T{t}", [128, 8, 128], BF16) for t in range(TG)]
    qT = sb("d_qT", [128, 16, 128], BF16)
    s12 = sb("d_s12", [128, 16, 128], F32)
    wk = sb("d_wk", [128, 256], F32)
    t16 = sb("d_t16", [128, 16, 16], F32)
    cand = sb("d_cand", [128, 8, 256], F32)
    b16 = sb("d_b16", [128, 8, 16], F32)
    dd = sb("d_dd", [128, 8, 16], F32)
    ee = sb("d_ee", [128, 8, 16], F32)
    zz = sb("d_zz", [128, 8], F32)
    lz = sb("d_lz", [128, 8], F32)
    mz = sb("d_mz", [128, 8], F32)
    th = sb("d_th", [128, 8], F32)
    a1 = sb("d_a1", [128, 8, 128], F32)
    G = [sb(f"d_G{t}", [128, 16384], BF16) for t in range(TG)]
    xx = [sb(f"d_xx{i}", [128, IB * 128], F32) for i in range(2)]
    ex = [sb(f"d_ex{i}", [128, IB * 128], BF16) for i in range(2)]
    mt = [sb(f"d_mt{i}", [128, IB * 128], BF16) for i in range(2)]
    ub = [sb(f"d_ub{i}", [128, 8, NB], BF16) for i in range(2)]
    vb = [sb(f"d_vb{i}", [128, NB // 128, 1024], BF16) for i in range(2)]
    ga = [sb(f"d_ga{i}", [128, NB], BF16) for i in range(2)]
    Wm = [sb(f"d_W{i}", [128, NB], BF16) for i in range(2)]
    WT = [sb(f"d_WT{i}", [128, NB // 128, 128], BF16) for i in range(2)]
    acc = [ps(f"d_acc{t}", [128, 1024], F32) for t in range(TG)]
    Ap = [ps(f"d_Ap{i}", [128, 512], F32) for i in range(2)]
    WTp = ps("d_WTp", [128, 1024], BF16)
    misc = ps("d_misc", [128, 512], F32)
    miscb = misc[:].bitcast(BF16)
    mixT_v = c.mixT_s.rearrange("(cc p) t -> p cc t", p=128)
    uTb_v = c.uT_bf.rearrange("(kc p) n -> p kc n", p=128)
    vb_v = c.v_bf.rearrange("(a p) d -> p a d", p=128)
    NCH = NB // 128
    ngroups = ngroups if ngroups is not None else NQ // 128 // TG
    blk = 0
    for g in range(ngroups):
        for ti in range(TG):
            tok0 = (g * TG + ti) * 128
            P.dma(mx[:], mixT_v[:, :, tok0:tok0 + 128], w=["mx"], sem="mx")
            P.dma(xt[:], c.xtok[tok0:tok0 + 128, :], w=["xt"], sem="xt")
            mp = acc[ti]
            for half in range(2):
                for cc in range(8):
                    P.op("pe", lambda e, half=half, cc=cc, mp=mp: e.matmul(mp[:, half * 512:(half + 1) * 512], mx[:, cc, :],
                                                                          wo[:, cc, half * 512:(half + 1) * 512], start=(cc == 0), stop=(cc == 7)),
                         r=["mx", WKEYS[0]], w=[f"acc{ti}"])
            for half in range(2):
                hs = slice(half * 512, (half + 1) * 512)
                P.op("dve", lambda e, mp=mp, ti=ti, hs=hs: e.scalar_tensor_tensor(out=x1[ti][:, hs], in0=xt[:, hs], scalar=ALPHA, in1=mp[:, hs], op0=ALU.mult, op1=ALU.add),
                     r=["xt", f"acc{ti}"], w=[f"x1_{ti}"])
            layernorm_rows(ph, "ln1", x1[ti][:], lnp[:, 0, :], lnp[:, 1, :], x1[ti][:], [f"x1_{ti}"], f"x1_{ti}")
            P.op("act", lambda e, ti=ti: e.activation(out=x1b[:], in_=x1[ti][:], func=AF.Copy), r=[f"x1_{ti}"], w=["x1b"])
            for kc in range(8):
                P.op("pe", lambda e, kc=kc: e.transpose(out=WTp[:, kc * 128:(kc + 1) * 128], in_=x1b[:, kc * 128:(kc + 1) * 128], identity=idb[:]),
                     r=["x1b", "idb"], w=["WTp"])
            P.op("dve", lambda e, ti=ti: e.tensor_copy(out=x1T[ti][:], in_=WTp[:].rearrange("p (k t) -> p k t", k=8)), r=["WTp"], w=[f"x1T{ti}", "WTp"])
            for qg in range(4):
                ws = wqn % 2
                wqn += 1
                P.dma(wqs[ws][:], wqb_v[:, :, qg * 512:(qg + 1) * 512], w=[f"wqs{ws}"], sem=f"wqs{ws}")
                for j in range(4):
                    for kc in range(8):
                        P.op("pe", lambda e, j=j, kc=kc, ti=ti, ws=ws: e.matmul(misc[:, j * 128:(j + 1) * 128], wqs[ws][:, kc, j * 128:(j + 1) * 128],
                                                                                x1T[ti][:, kc, :], start=(kc == 0), stop=(kc == 7)),
                             r=[f"wqs{ws}", f"x1T{ti}"], w=["misc"])
                P.op("act", lambda e, qg=qg: e.activation(out=qT[:, qg * 4:(qg + 1) * 4, :], in_=misc[:].rearrange("p (j t) -> p j t", j=4), func=AF.Copy),
                     r=["misc"], w=[f"qT{qg}", "misc"])
            for qg in range(4):
                for j in range(4):
                    ch = qg * 4 + j
                    hh, half = ch // 2, ch % 2
                    P.op("pe", lambda e, ch=ch, j=j, hh=hh, half=half: e.matmul(misc[:, j * 128:(j + 1) * 128], qT[:, ch, :], kT[half][:, hh, :],
                                                                                start=True, stop=True),
                         r=[f"qT{qg}", f"kT{half}"], w=["misc"])
                P.op("dve", lambda e, qg=qg: e.tensor_copy(out=s12[:, qg * 4:(qg + 1) * 4, :], in_=misc[:].rearrange("p (j t) -> p j t", j=4)),
                     r=["misc"], w=[f"s12_{qg}", "misc"])
            SK = [f"s12_{q}" for q in range(4)]
            for ch in range(16):
                P.op("dve", lambda e, ch=ch: e.max(out=t16[:, ch, 0:8], in_=s12[:, ch, :]), r=SK, w=[f"t16a{ch}"])
                P.op("dve", lambda e, ch=ch: e.match_replace(out=wk[:, 0:128], in_to_replace=t16[:, ch, 0:8], in_values=s12[:, ch, :], imm_value=-1e30),
                     r=SK + [f"t16a{ch}"], w=["wk"])
                P.op("dve", lambda e, ch=ch: e.max(out=t16[:, ch, 8:16], in_=wk[:, 0:128]), r=["wk"], w=[f"t16b{ch}"])
            TK = [f"t16a{ch}" for ch in range(16)] + [f"t16b{ch}" for ch in range(16)]
            in0 = mkap(t16[:, 0, :], [[32, 8], [1, 16], [0, 16]])
            in1 = mkap(t16[:, 1, :], [[32, 8], [0, 16], [1, 16]])
            P.op("pool", lambda e, in0=in0, in1=in1: e.tensor_tensor(out=cand[:].rearrange("p h (a b) -> p h a b", a=16), in0=in0, in1=in1, op=ALU.add),
                 r=TK, w=["cand"])
            for hh in range(8):
                P.op("dve", lambda e, hh=hh: e.max(out=b16[:, hh, 0:8], in_=cand[:, hh, :]), r=["cand"], w=[f"b16a{hh}"])
                P.op("dve", lambda e, hh=hh: e.match_replace(out=wk[:], in_to_replace=b16[:, hh, 0:8], in_values=cand[:, hh, :], imm_value=-1e30),
                     r=["cand", f"b16a{hh}"], w=["wk"])
                P.op("dve", lambda e, hh=hh: e.max(out=b16[:, hh, 8:16], in_=wk[:]), r=["wk"], w=[f"b16b{hh}"])
            BK = [f"b16a{hh}" for hh in range(8)] + [f"b16b{hh}" for hh in range(8)]
            m_b = mkap(b16[:, 0, 0:1], [[16, 8], [0, 16]])
            P.op("pool", lambda e, m_b=m_b: e.tensor_tensor(out=dd[:], in0=b16[:], in1=m_b, op=ALU.subtract), r=BK, w=["dd"])
            P.op("act", lambda e: e.activation(out=ee[:], in_=dd[:], func=AF.Exp), r=["dd"], w=["ee"])
            P.op("dve", lambda e: e.tensor_reduce(out=zz[:], in_=ee[:], axis=AX.X, op=ALU.add), r=["ee"], w=["zz"])
            P.op("act", lambda e: e.activation(out=lz[:], in_=zz[:], func=AF.Ln), r=["zz"], w=["lz"])
            m_v = mkap(b16[:, 0, 0:1], [[16, 8]])
            t_v = mkap(b16[:, 0, 15:16], [[16, 8]])
            P.op("pool", lambda e, m_v=m_v: e.tensor_tensor(out=mz[:], in0=m_v, in1=lz[:], op=ALU.add), r=BK + ["lz"], w=["mz"])
            P.op("pool", lambda e, t_v=t_v: e.tensor_tensor(out=th[:], in0=t_v, in1=mz[:], op=ALU.subtract), r=BK + ["mz"], w=["th"])
            s1_v = mkap(s12[:, 0, :], [[256, 8], [1, 128]])
            mz_b = mkap(mz[:, 0:1], [[1, 8], [0, 128]])
            P.op("pool", lambda e, s1_v=s1_v, mz_b=mz_b: e.tensor_tensor(out=a1[:], in0=s1_v, in1=mz_b, op=ALU.subtract), r=SK + ["mz"], w=["a1"])
            for ib in range(128 // IB):
                for hh in range(8):
                    bs = blk % 2
                    blk += 1
                    a_b = mkap(a1[:, hh, ib * IB:(ib + 1) * IB], [[1, IB], [0, 128]])
                    s_b = mkap(s12[:, 2 * hh + 1, :], [[0, IB], [1, 128]])
                    P.op("pool", lambda e, a_b=a_b, s_b=s_b, bs=bs: e.tensor_tensor(out=xx[bs][:].rearrange("p (i j) -> p i j", i=IB), in0=a_b, in1=s_b, op=ALU.add),
                         r=["a1"] + SK, w=[f"xx{bs}"])
                    P.op("act", lambda e, bs=bs: e.activation(out=ex[bs][:], in_=xx[bs][:], func=AF.Exp), r=[f"xx{bs}"], w=[f"ex{bs}"])
                    gsl = G[ti][:, ib * IB * 128:(ib + 1) * IB * 128]
                    if hh == 0:
                        P.op("dve", lambda e, bs=bs, hh=hh, gsl=gsl: e.scalar_tensor_tensor(out=gsl, in0=xx[bs][:], scalar=th[:, hh:hh + 1], in1=ex[bs][:],
                                                                                           op0=ALU.is_ge, op1=ALU.mult),
                             r=[f"xx{bs}", f"ex{bs}", "th"], w=[f"G{ti}_{ib}"])
                    else:
                        P.op("dve", lambda e, bs=bs, hh=hh: e.scalar_tensor_tensor(out=mt[bs][:], in0=xx[bs][:], scalar=th[:, hh:hh + 1], in1=ex[bs][:],
                                                                                  op0=ALU.is_ge, op1=ALU.mult),
                             r=[f"xx{bs}", f"ex{bs}", "th"], w=[f"mt{bs}"])
                        P.op("dve", lambda e, bs=bs, gsl=gsl: e.tensor_tensor(out=gsl, in0=gsl, in1=mt[bs][:], op=ALU.add),
                             r=[f"mt{bs}", f"G{ti}_{ib}"], w=[f"G{ti}_{ib}"])
        nblk = 16384 // NB
        for nb in range(nblk):
            s = nb % 2
            P.dma(ub[s][:], uTb_v[:, :, nb * NB:(nb + 1) * NB], w=[f"ub{s}"], sem=f"ub{s}")
            P.dma(vb[s][:], vb_v[:, nb * NCH:(nb + 1) * NCH, :], w=[f"vb{s}"], sem=f"vb{s}")
            for ti in range(TG):
                k = (nb * TG + ti) % 2
                for kc in range(8):
                    P.op("pe", lambda e, k=k, kc=kc, ti=ti, s=s: e.matmul(Ap[k][:, 0:NB], x1T[ti][:, kc, :], ub[s][:, kc, :], start=(kc == 0), stop=(kc == 7)),
                         r=[f"x1T{ti}", f"ub{s}"], w=[f"Ap{k}"])
                P.op("act", lambda e, k=k: e.activation(out=ga[k][:], in_=Ap[k][:, 0:NB], func=AF.Gelu), r=[f"Ap{k}"], w=[f"ga{k}"])
                gkeys = [f"G{ti}_{ib}" for ib in range((nb * NB) // (IB * 128), ((nb + 1) * NB - 1) // (IB * 128) + 1)]
                P.op("dve", lambda e, k=k, ti=ti, nb=nb: e.tensor_tensor(out=Wm[k][:], in0=ga[k][:], in1=G[ti][:, nb * NB:(nb + 1) * NB], op=ALU.mult),
                     r=[f"ga{k}"] + gkeys, w=[f"W{k}"])
                wtp = WTp if k == 0 else miscb
                wkey = "WTp" if k == 0 else "misc"
                for ch in range(NCH):
                    P.op("pe", lambda e, k=k, ch=ch, wtp=wtp: e.transpose(out=wtp[:, ch * 128:(ch + 1) * 128], in_=Wm[k][:, ch * 128:(ch + 1) * 128], identity=idb[:]),
                         r=[f"W{k}", "idb"], w=[wkey])
                P.op("act", lambda e, k=k, wtp=wtp: e.activation(out=WT[k][:], in_=wtp[:, 0:NCH * 128].rearrange("p (c t) -> p c t", c=NCH), func=AF.Copy),
                     r=[wkey], w=[f"WT{k}", wkey])
                for ch in range(NCH):
                    for half in range(2):
                        P.op("pe", lambda e, k=k, ch=ch, half=half, ti=ti, s=s, nb=nb: e.matmul(acc[ti][:, half * 512:(half + 1) * 512], WT[k][:, ch, :],
                                                                                                  vb[s][:, ch, half * 512:(half + 1) * 512],
                                                                                                  start=(nb == 0 and ch == 0), stop=(nb == nblk - 1 and ch == NCH - 1)),
                             r=[f"WT{k}", f"vb{s}"], w=[f"acc{ti}"])
        for ti in range(TG):
            tok0 = (g * TG + ti) * 128
            for half in range(2):
                hs = slice(half * 512, (half + 1) * 512)
                P.op("dve", lambda e, ti=ti, hs=hs: e.scalar_tensor_tensor(out=xt[:, hs], in0=x1[ti][:, hs], scalar=ALPHA, in1=acc[ti][:, hs], op0=ALU.mult, op1=ALU.add),
                     r=[f"x1_{ti}", f"acc{ti}"], w=["xt"])
            layernorm_rows(ph, "ln2", xt[:], lnp[:, 2, :], lnp[:, 3, :], xt[:], ["xt"], "xt")
            ph.store(c.y[tok0:tok0 + 128, :], xt[:], r=["xt"], sem="yo")
    ph.finish()


def ln_inplace(ph, buf, lnp, key):
    P = ph.P
    st, mv, sd, rs = ph.t["st"], ph.t["mv"], ph.t["sd"], ph.t["rs"]
    P.op("dve", lambda e: e.bn_stats(out=st[:, 0, :], in_=buf[:, 0:512]), r=[key], w=["st0"])
    P.op("dve", lambda e: e.bn_stats(out=st[:, 1, :], in_=buf[:, 512:1024]), r=[key], w=["st1"])
    P.op("dve", lambda e: e.bn_aggr(out=mv[:], in_=st[:].rearrange("p a b -> p (a b)")), r=["st0", "st1"], w=["mv"])
    P.op("act", lambda e: e.activation(out=sd[:], in_=mv[:, 1:2], func=AF.Sqrt, bias=LN_EPS, scale=1.0), r=["mv"], w=["sd"])
    P.op("dve", lambda e: e.reciprocal(out=rs[:], in_=sd[:]), r=["sd"], w=["rs"])
    P.op("dve", lambda e: e.tensor_scalar(out=buf, in0=buf, scalar1=mv[:, 0:1], scalar2=rs[:, 0:1], op0=ALU.subtract, op1=ALU.mult),
         r=[key, "mv", "rs"], w=[key])
    P.op("dve", lambda e: e.tensor_tensor(out=buf, in0=buf, in1=lnp[:, 0, :], op=ALU.mult), r=[key, "lnp"], w=[key])
    P.op("dve", lambda e: e.tensor_tensor(out=buf, in0=buf, in1=lnp[:, 1, :], op=ALU.add), r=[key, "lnp"], w=[key])


def phase_D2(nc, c, ngroups=None):
    TG = 2
    ph = Phase(nc)
    P = ph.P
    sb, ps = ph.sb, ph.ps
    ph.t = {}
    arena = sb("d_arena", [128, 4096], F32)
    ar2 = arena[:, 2048:4096].bitcast(BF16)
    cand = arena[:, 0:2048].rearrange("p (h c) -> p h c", h=8)
    y200 = [arena[:, s * 1024:(s + 1) * 1024] for s in range(2)]
    qT = ar2[:, 0:2048].rearrange("p (c t) -> p c t", c=16)
    mx = ar2[:, 2048:3072].rearrange("p (c t) -> p c t", c=8)
    x1b = ar2[:, 3072:4096]
    OHc = [ar2[:, s * 2048:(s + 1) * 2048] for s in range(2)]
    idf = sb("d_idf", [128, 128], F32)
    idb = sb("d_idb", [128, 128], BF16)
    P.dma(idf[:], c.ident[:, :], w=["idf"], sem="idf")
    cast(P, "dve", idb[:], idf[:], r=["idf"], w=["idb"])
    kT = [sb(f"d_k{j}T", [128, 8, 128], BF16) for j in range(2)]
    for j, src in enumerate((c.k1T, c.k2T)):
        sv = arena[:, 0:1024].rearrange("p (h n) -> p h n", h=8)
        P.dma(sv, src[:, :, :], w=["arena"], sem="stage")
        cast(P, "dve", kT[j][:], sv, r=["arena"], w=[f"kT{j}", "arena"])
    ring = [sb(f"d_ring{i}", [128, 8, 256], BF16) for i in range(2)]
    lnp = sb("d_lnp", [128, 2, 1024], F32)
    ph.t["st"] = sb("d_st", [128, 2, 6], F32)
    ph.t["mv"] = sb("d_mv", [128, 2], F32)
    ph.t["sd"] = sb("d_sd", [128, 1], F32)
    ph.t["rs"] = sb("d_rs", [128, 1], F32)
    xt = sb("d_xt", [128, 1024], F32)
    x1T = sb("d_x1T", [128, 8, TG * 128], BF16)
    s12 = sb("d_s12", [128, 16, 128], F32)
    wk = sb("d_wk", [128, 256], F32)
    t16 = sb("d_t16", [128, 16, 16], F32)
    b16 = sb("d_b16", [128, 8, 16], F32)
    dd = sb("d_dd", [128, 8, 16], F32)
    ee = sb("d_ee", [128, 8, 16], F32)
    zz = sb("d_zz", [128, 8], F32)
    lz = sb("d_lz", [128, 8], F32)
    mz = sb("d_mz", [128, 8], F32)
    th = sb("d_th", [128, 8], F32)
    cex = sb("d_cex", [128, 8], F32)
    cT = sb("d_cT", [128, 128], F32)
    thr2 = sb("d_thr2", [128, 8], F32)
    bb2 = sb("d_bb2", [128, 8, 16], F32)
    msk = [sb(f"d_msk{i}", [128, 1024], BF16) for i in range(2)]
    eex = [sb(f"d_eex{i}", [128, 1024], BF16) for i in range(2)]
    Mc = [sb(f"d_Mc{i}", [128, 1024], BF16) for i in range(2)]
    OHT = sb("d_OHT", [128, 128, 128], BF16)
    MT = sb("d_MT", [128, 128, 64], BF16)
    G = sb("d_G", [128, TG * 128, 128], BF16)
    ub = [sb(f"d_ub{i}", [128, 8, 256], BF16) for i in range(3)]
    vb = [sb(f"d_vb{i}", [128, 2, 1024], BF16) for i in range(3)]
    ga = [sb(f"d_ga{i}", [128, TG * 128], BF16) for i in range(2)]
    Wj = [sb(f"d_Wj{i}", [128, TG * 128], BF16) for i in range(2)]
    acc = [ps(f"d_acc{t}", [128, 1024], F32) for t in range(TG)]
    Ap = [ps(f"d_Ap{i}", [128, 512], F32) for i in range(2)]
    WTp = ps("d_WTp", [128, 1024], BF16)
    misc = ps("d_misc", [128, 512], F32)
    Apb = [Ap[i][:].bitcast(BF16) for i in range(2)]
    miscb = misc[:].bitcast(BF16)
    mixT_v = c.mixT_s.rearrange("(cc p) t -> p cc t", p=128)
    uTb_v = c.uT_bf.rearrange("(kc p) n -> p kc n", p=128)
    vb_v = c.v_bf.rearrange("(a p) d -> p a d", p=128)
    wob_v = c.wo_bf.rearrange("(kc p) n -> p kc n", p=128)
    wqb_v = c.wq_bf.rearrange("(kc p) n -> p kc n", p=128)
    ngroups = ngroups if ngroups is not None else NQ // 128 // TG
    preloaded = set()
    epb = [arena[:, 0:1024], arena[:, 1024:2048]]
    rn = 0
    SK = [f"s12_{q}" for q in range(4)]
    TK = [f"t16a{ch}" for ch in range(16)] + [f"t16b{ch}" for ch in range(16)]
    BK = [f"b16a{hh}" for hh in range(8)] + [f"b16b{hh}" for hh in range(8)]
    och = 0
    mch = 0
    gev = 0
    for g in range(ngroups):
        for ti in range(TG):
            tok0 = (g * TG + ti) * 128
            if tok0 not in preloaded:
                P.dma(mx, mixT_v[:, :, tok0:tok0 + 128], w=["mx", "arena"], sem="mx")
                P.dma(xt[:], c.xtok[tok0:tok0 + 128, :], w=["xt"], sem="xt")
            P.dma(lnp[:, 0, :], c.ln1_g[0:1, :].partition_broadcast(128), w=["lnp"], sem="lnp")
            P.dma(lnp[:, 1, :], c.ln1_b[0:1, :].partition_broadcast(128), w=["lnp"], sem="lnp")
            mp = acc[ti]
            for dq in range(4):
                rs_ = rn % 2
                rn += 1
                P.dma(ring[rs_][:], wob_v[:, :, dq * 256:(dq + 1) * 256], w=[f"ring{rs_}"], sem=f"ring{rs_}")
                for cc in range(8):
                    P.op("pe", lambda e, dq=dq, cc=cc, mp=mp, rs_=rs_: e.matmul(mp[:, dq * 256:(dq + 1) * 256], mx[:, cc, :], ring[rs_][:, cc, :],
                                                                                start=(cc == 0), stop=(cc == 7)),
                         r=["mx", f"ring{rs_}", f"acc{ti}_0", f"acc{ti}_1"], w=[f"acc{ti}"])
            for half in range(2):
                hs = slice(half * 512, (half + 1) * 512)
                P.op("dve", lambda e, mp=mp, hs=hs: e.scalar_tensor_tensor(out=xt[:, hs], in0=xt[:, hs], scalar=ALPHA, in1=mp[:, hs], op0=ALU.mult, op1=ALU.add),
                     r=["xt", f"acc{ti}"], w=["xt", f"acc{ti}"])
            ln_inplace(ph, xt[:], lnp, "xt")
            ph.stores.append(P.dma(c.x1_s[tok0:tok0 + 128, :], xt[:], r=["xt"], w=[f"x1s{ti}"], sem="x1s"))
            P.op("act", lambda e: e.activation(out=x1b, in_=xt[:], func=AF.Copy), r=["xt"], w=["x1b", "arena"])
            for kc in range(8):
                P.op("pe", lambda e, kc=kc: e.transpose(out=WTp[:, kc * 128:(kc + 1) * 128], in_=x1b[:, kc * 128:(kc + 1) * 128], identity=idb[:]),
                     r=["x1b", "idb"], w=["WTp"])
            P.op("dve", lambda e, ti=ti: e.tensor_copy(out=x1T[:, :, ti * 128:(ti + 1) * 128], in_=WTp[:].rearrange("p (k t) -> p k t", k=8)),
                 r=["WTp"], w=[f"x1T{ti}", "WTp"])
            for qg in range(8):
                rs_ = rn % 2
                rn += 1
                P.dma(ring[rs_][:], wqb_v[:, :, qg * 256:(qg + 1) * 256], w=[f"ring{rs_}"], sem=f"ring{rs_}")
                for j in range(2):
                    for kc in range(8):
                        P.op("pe", lambda e, j=j, kc=kc, ti=ti, rs_=rs_: e.matmul(misc[:, j * 128:(j + 1) * 128], ring[rs_][:, kc, j * 128:(j + 1) * 128],
                                                                                  x1T[:, kc, ti * 128:(ti + 1) * 128], start=(kc == 0), stop=(kc == 7)),
                             r=[f"ring{rs_}", f"x1T{ti}"], w=["misc"])
                P.op("act", lambda e, qg=qg: e.activation(out=qT[:, qg * 2:(qg + 1) * 2, :], in_=misc[:, 0:256].rearrange("p (j t) -> p j t", j=2), func=AF.Copy),
                     r=["misc"], w=[f"qT{qg}", "misc", "arena"])
            for qg in range(4):
                for j in range(4):
                    ch = qg * 4 + j
                    hh, half = ch // 2, ch % 2
                    P.op("pe", lambda e, ch=ch, j=j, hh=hh, half=half: e.matmul(misc[:, j * 128:(j + 1) * 128], qT[:, ch, :], kT[half][:, hh, :],
                                                                                start=True, stop=True),
                         r=[f"qT{ch // 2}", f"kT{half}"], w=["misc"])
                P.op("dve", lambda e, qg=qg: e.tensor_copy(out=s12[:, qg * 4:(qg + 1) * 4, :], in_=misc[:].rearrange("p (j t) -> p j t", j=4)),
                     r=["misc"], w=[f"s12_{qg}", "misc"])
            mskf = [msk[i][:].bitcast(F32) for i in range(2)]
            scr1 = [(wk[:, 0:128], ["wk0"]), (wk[:, 128:256], ["wk1"]), (mskf[0][:, 0:128], ["msk0"]), (mskf[1][:, 0:128], ["msk1"])]
            for c4 in range(4):
                grp = [4 * c4 + q for q in range(4)]
                for q, ch in enumerate(grp):
                    P.op("dve", lambda e, ch=ch: e.max(out=t16[:, ch, 0:8], in_=s12[:, ch, :]), r=SK, w=[f"t16a{ch}"])
                for q, ch in enumerate(grp):
                    sc, sk = scr1[q]
                    P.op("dve", lambda e, ch=ch, sc=sc: e.match_replace(out=sc, in_to_replace=t16[:, ch, 0:8], in_values=s12[:, ch, :], imm_value=-1e30),
                         r=SK + [f"t16a{ch}"], w=sk)
                for q, ch in enumerate(grp):
                    sc, sk = scr1[q]
                    P.op("dve", lambda e, ch=ch, sc=sc: e.max(out=t16[:, ch, 8:16], in_=sc), r=sk, w=[f"t16b{ch}"])
            in0 = mkap(t16[:, 0, :], [[32, 8], [1, 16], [0, 16]])
            in1 = mkap(t16[:, 1, :], [[32, 8], [0, 16], [1, 16]])
            P.op("dve", lambda e, in0=in0, in1=in1: e.tensor_tensor(out=cand.rearrange("p h (a b) -> p h a b", a=16), in0=in0, in1=in1, op=ALU.add),
                 r=TK, w=["cand", "arena", "epb0", "epb1"])
            scr2 = [(wk[:], ["wk0", "wk1"]), (mskf[0][:, 0:256], ["msk0"]), (mskf[1][:, 0:256], ["msk1"])]
            for grp in ((0, 1, 2), (3, 4, 5), (6, 7)):
                for q, hh in enumerate(grp):
                    P.op("dve", lambda e, hh=hh: e.max(out=b16[:, hh, 0:8], in_=cand[:, hh, :]), r=["cand"], w=[f"b16a{hh}"])
                for q, hh in enumerate(grp):
                    sc, sk = scr2[q]
                    P.op("dve", lambda e, hh=hh, sc=sc: e.match_replace(out=sc, in_to_replace=b16[:, hh, 0:8], in_values=cand[:, hh, :], imm_value=-1e30),
                         r=["cand", f"b16a{hh}"], w=sk)
                for q, hh in enumerate(grp):
                    sc, sk = scr2[q]
                    P.op("dve", lambda e, hh=hh, sc=sc: e.max(out=b16[:, hh, 8:16], in_=sc), r=sk, w=[f"b16b{hh}"])
            m_b = mkap(b16[:, 0, 0:1], [[16, 8], [0, 16]])
            P.op("pool", lambda e, m_b=m_b: e.tensor_tensor(out=dd[:], in0=b16[:], in1=m_b, op=ALU.subtract), r=BK, w=["dd"])
            P.op("act", lambda e: e.activation(out=ee[:], in_=dd[:], func=AF.Exp), r=["dd"], w=["ee"])
            P.op("dve", lambda e: e.tensor_reduce(out=zz[:], in_=ee[:], axis=AX.X, op=ALU.add), r=["ee"], w=["zz"])
            P.op("act", lambda e: e.activation(out=lz[:], in_=zz[:], func=AF.Ln), r=["zz"], w=["lz"])
            m_v = mkap(b16[:, 0, 0:1], [[16, 8]])
            t_v = mkap(b16[:, 0, 15:16], [[16, 8]])
            P.op("pool", lambda e, m_v=m_v: e.tensor_tensor(out=mz[:], in0=m_v, in1=lz[:], op=ALU.add), r=BK + ["lz"], w=["mz"])
            P.op("pool", lambda e, t_v=t_v: e.tensor_tensor(out=th[:], in0=t_v, in1=mz[:], op=ALU.subtract), r=BK + ["mz"], w=["th"])
            P.op("pool", lambda e: e.tensor_scalar(out=thr2[:], in0=mz[:], scalar1=-200.0, scalar2=None, op0=ALU.add), r=["mz"], w=["thr2"])
            P.op("pool", lambda e: e.tensor_scalar(out=cex[:], in0=th[:], scalar1=200.0, scalar2=None, op0=ALU.add), r=["th"], w=["cex"])
            v1_v = mkap(t16[:, 0, :], [[32, 8], [1, 16]])
            P.op("pool", lambda e, v1_v=v1_v: e.tensor_tensor(out=bb2[:], in0=v1_v, in1=mkap(thr2[:, 0:1], [[1, 8], [0, 16]]), op=ALU.subtract),
                 r=TK + ["thr2"], w=["bb2"])
            def oh_a(ic):
                s_ = ic % 2
                s1_b = mkap(s12[:, 0, ic * 16:(ic + 1) * 16], [[256, 8], [0, 16], [1, 16]])
                v1_b = mkap(t16[:, 0, :], [[32, 8], [1, 16], [0, 16]])
                P.op("dve", lambda e: e.tensor_tensor(out=OHc[s_].rearrange("p (h r i) -> p h r i", h=8, r=16), in0=s1_b, in1=v1_b, op=ALU.is_equal),
                     r=SK + TK + ["arena"], w=[f"OHc{s_}"])

            def oh_b(ic):
                s_ = ic % 2
                o3 = OHc[s_].rearrange("p (q i) -> p q i", i=16)
                for ii in range(16):
                    bk = ii // 8
                    P.op("pe", lambda e, ii=ii, bk=bk: e.transpose(out=Apb[bk][:, (ii % 8) * 128:(ii % 8 + 1) * 128], in_=o3[:, :, ii], identity=idb[:]),
                         r=[f"OHc{s_}", "idb"], w=[f"Ap{bk}"])
                for bk in range(2):
                    i0 = ic * 16 + bk * 8
                    ov = OHT[:, :, i0:i0 + 8]
                    P.op("act", lambda e, bk=bk, ov=ov: e.activation(out=ov, in_=mkap(Apb[bk][:, 0:1], [[1, 128], [128, 8]]), func=AF.Copy),
                         r=[f"Ap{bk}"], w=["OHT", f"Ap{bk}"])

            def m_a(jh, jc):
                s_ = jc % 2
                j0 = jh * 64 + jc * 8
                s2_b = mkap(s12[:, 1, j0:j0 + 8], [[256, 8], [0, 16], [1, 8]])
                bb_b = mkap(bb2[:, 0, :], [[16, 8], [1, 16], [0, 8]])
                P.op("pool", lambda e: e.tensor_tensor(out=y200[s_].rearrange("p (h r j) -> p h r j", h=8, r=16), in0=s2_b, in1=bb_b, op=ALU.add),
                     r=SK + ["bb2", "arena"], w=[f"y200{s_}"])
                P.op("dve", lambda e: e.tensor_tensor(out=msk[s_][:].rearrange("p (h q) -> p h q", h=8), in0=y200[s_].rearrange("p (h q) -> p h q", h=8),
                                                      in1=mkap(cex[:, 0:1], [[1, 8], [0, 128]]), op=ALU.is_ge),
                     r=[f"y200{s_}", "cex"], w=[f"msk{s_}"])
                P.op("act", lambda e: e.activation(out=eex[s_][:], in_=y200[s_], func=AF.Exp, bias=-200.0, scale=1.0),
                     r=[f"y200{s_}"], w=[f"eex{s_}"])
                P.op("dve", lambda e: e.tensor_tensor(out=Mc[s_][:], in0=msk[s_][:], in1=eex[s_][:], op=ALU.mult),
                     r=[f"msk{s_}", f"eex{s_}"], w=[f"Mc{s_}"])

            def m_b(jh, jc):
                s_ = jc % 2
                m3 = Mc[s_][:].rearrange("p (q j) -> p q j", j=8)
                tb = WTp if s_ == 0 else miscb
                tkey = "WTp" if s_ == 0 else "misc"
                for jj in range(8):
                    P.op("pe", lambda e, jj=jj: e.transpose(out=tb[:, jj * 128:(jj + 1) * 128], in_=m3[:, :, jj], identity=idb[:]),
                         r=[f"Mc{s_}", "idb"], w=[tkey])
                mv_ = MT[:, :, jc * 8:jc * 8 + 8]
                P.op("act", lambda e: e.activation(out=mv_, in_=mkap(tb[:, 0:1], [[1, 128], [128, 8]]), func=AF.Copy),
                     r=[tkey], w=["MT", tkey])

            def scatter(jh):
                nonlocal gev
                gacc = acc[1 - ti]
                for t8 in range(16):
                    bk = gev % 2
                    gev += 1
                    for tt in range(8):
                        t = t8 * 8 + tt
                        P.op("pe", lambda e, t=t, tt=tt, bk=bk: e.matmul(gacc[:, bk * 512 + tt * 64: bk * 512 + (tt + 1) * 64], OHT[:, t, :], MT[:, t, :],
                                                                        start=True, stop=True),
                             r=["OHT", "MT", f"acc{1 - ti}"], w=[f"acc{1 - ti}_{bk}"])
                    gv = mkap(G[:, ti * 128 + t8 * 8, jh * 64:(jh + 1) * 64], [[128, 8], [1, 64]])
                    gsrc = gacc[:, bk * 512:(bk + 1) * 512].rearrange("p (t j) -> p t j", t=8)
                    if t8 % 2 == 0:
                        P.op("act", lambda e, gv=gv, gsrc=gsrc: e.activation(out=gv, in_=gsrc, func=AF.Copy),
                             r=[f"acc{1 - ti}_{bk}"], w=[f"G{ti}{jh}", f"acc{1 - ti}_{bk}"])
                    else:
                        P.op("dve", lambda e, gv=gv, gsrc=gsrc: e.tensor_copy(out=gv, in_=gsrc),
                             r=[f"acc{1 - ti}_{bk}"], w=[f"G{ti}{jh}", f"acc{1 - ti}_{bk}"])

            oh_a(0)
            m_a(0, 0)
            for step in range(8):
                if step + 1 < 8:
                    oh_a(step + 1)
                    m_a(0, step + 1)
                oh_b(step)
                m_b(0, step)
            m_a(1, 0)
            scatter(0)
            for step in range(8):
                if step + 1 < 8:
                    m_a(1, step + 1)
                m_b(1, step)
            scatter(1)
        GK = [f"G{ti}{jh}" for ti in range(TG) for jh in range(2)]
        AK = [f"acc{t}_{b}" for t in range(TG) for b in range(2)]
        for ti in range(TG):
            tok0 = (g * TG + ti) * 128
            P.dma(epb[ti], c.x1_s[tok0:tok0 + 128, :], r=[f"x1s{ti}"], w=[f"epb{ti}", "cand", "y2000", "y2001", "arena"], sem=f"epb{ti}")
        P.dma(lnp[:, 0, :], c.ln2_g[0:1, :].partition_broadcast(128), w=["lnp"], sem="lnp")
        P.dma(lnp[:, 1, :], c.ln2_b[0:1, :].partition_broadcast(128), w=["lnp"], sem="lnp")
        NSL = 3

        def load_uv(jb):
            sl = jb % NSL
            P.dma(ub[sl][:], uTb_v[:, :, jb * 256:(jb + 1) * 256], w=[f"ub{sl}"], sem=f"ub{sl}")
            P.dma(vb[sl][:], vb_v[:, jb * 2:(jb + 1) * 2, :], w=[f"vb{sl}"], sem=f"vb{sl}")

        def issue_A(j):
            jb, jj = j // 2, j % 2
            sl, k = jb % NSL, j % 2
            if jj == 0 and jb + 1 < 64:
                load_uv(jb + 1)
            for kc in range(8):
                P.op("pe", lambda e, kc=kc: e.matmul(Ap[k][:, 0:TG * 128], ub[sl][:, kc, jj * 128:(jj + 1) * 128], x1T[:, kc, :],
                                                     start=(kc == 0), stop=(kc == 7)),
                     r=[f"x1T{t}" for t in range(TG)] + [f"ub{sl}"], w=[f"Ap{k}"])
            P.op("act", lambda e: e.activation(out=ga[k][:], in_=Ap[k][:, 0:TG * 128], func=AF.Gelu), r=[f"Ap{k}"], w=[f"ga{k}", f"Ap{k}"])
            gj = mkap(G[:, 0, j:j + 1], [[128, TG * 128]])
            P.op("dve", lambda e: e.tensor_tensor(out=Wj[k][:], in0=ga[k][:], in1=gj, op=ALU.mult),
                 r=[f"ga{k}"] + GK, w=[f"Wj{k}"])

        load_uv(0)
        issue_A(0)
        for j in range(128):
            if j + 1 < 128:
                issue_A(j + 1)
            jb, jj = j // 2, j % 2
            sl, k = jb % NSL, j % 2
            for ti in range(TG):
                for half in range(2):
                    P.op("pe", lambda e, k=k, ti=ti, half=half, sl=sl, jj=jj, j=j: e.matmul(acc[ti][:, half * 512:(half + 1) * 512], Wj[k][:, ti * 128:(ti + 1) * 128],
                                                                                             vb[sl][:, jj, half * 512:(half + 1) * 512],
                                                                                             start=(j == 0), stop=(j == 127)),
                         r=[f"Wj{k}", f"vb{sl}"] + ([f"acc{ti}_0", f"acc{ti}_1"] if j == 0 else []), w=[f"acc{ti}"])
        if g + 1 < ngroups:
            ntok = (g + 1) * TG * 128
            P.dma(mx, mixT_v[:, :, ntok:ntok + 128], w=["mx", "arena"], sem="mx")
            P.dma(xt[:], c.xtok[ntok:ntok + 128, :], w=["xt"], sem="xt")
            preloaded.add(ntok)
        for ti in range(TG):
            tok0 = (g * TG + ti) * 128
            eb = epb[ti]
            ek = f"epb{ti}"
            for half in range(2):
                hs = slice(half * 512, (half + 1) * 512)
                P.op("dve", lambda e, ti=ti, hs=hs, eb=eb: e.scalar_tensor_tensor(out=eb[:, hs], in0=eb[:, hs], scalar=ALPHA, in1=acc[ti][:, hs], op0=ALU.mult, op1=ALU.add),
                     r=[ek, f"acc{ti}"], w=[ek, f"acc{ti}"])
            ln_inplace(ph, eb, lnp, ek)
            ph.stores.append(P.dma(c.y[tok0:tok0 + 128, :], eb, r=[ek], w=[], sem="yo"))
    ph.finish()


def build_program():
    nc = bass.Bass("TRN2", target_bir_lowering=False)
    c = declare(nc)
    phase_A(nc, c)
    phase_A3(nc, c)
    phase_B(nc, c)
    phase_D2(nc, c)
    return nc


def _rope_table(pos):
    inv = (1.0 / (np.float32(10000.0) ** (np.arange(0, 32, 2, dtype=np.float32) / np.float32(32)))).astype(np.float32)
    ang = pos.astype(np.float32)[:, None] * inv[None, :]
    cs, sn = np.cos(ang).astype(np.float32), np.sin(ang).astype(np.float32)
    return np.ascontiguousarray(np.concatenate([cs, cs, -sn, sn], axis=1))


def make_in_maps(x_prompt, x_sample, w_in, conv_w, q_norm_g, w_uq, kv_norm_g, w_ukv, w_o,
                 ln1_g, ln1_b, peer_wq, peer_k1, peer_k2, peer_u, peer_v, ln2_g, ln2_b):
    f = lambda a: np.ascontiguousarray(np.asarray(a, dtype=np.float32))
    x_prompt, x_sample = f(x_prompt), f(x_sample)
    shared = {
        "ident": np.eye(128, dtype=np.float32),
        "w_in": f(w_in[0]), "conv_wT": f(np.asarray(conv_w[0]).T), "q_norm_g": f(q_norm_g), "w_uq": f(w_uq[0]),
        "kv_norm_g": f(kv_norm_g), "w_ukv": f(w_ukv[0]), "w_o": f(w_o[0]), "ln1_g": f(ln1_g), "ln1_b": f(ln1_b),
        "peer_wq": f(peer_wq[0]),
        "k1T": f(np.transpose(np.asarray(peer_k1[0]), (2, 0, 1))), "k2T": f(np.transpose(np.asarray(peer_k2[0]), (2, 0, 1))),
        "uT": f(np.asarray(peer_u[0]).reshape(128, 128, 1024).transpose(1, 0, 2).reshape(16384, 1024).T),
        "v": f(np.asarray(peer_v[0]).reshape(128, 128, 1024).transpose(1, 0, 2).reshape(16384, 1024)),
        "ln2_g": f(ln2_g), "ln2_b": f(ln2_b),
        "rope_kv": _rope_table(np.arange(16384)),
    }
    xTkv = [f(x_prompt[p].T) for p in range(2)]
    maps = []
    for c in range(NCORES):
        p, qr = c // 4, c % 4
        own = np.concatenate([x_prompt[p, qr * 4096:(qr + 1) * 4096], x_sample[c]], axis=0)
        halo = np.zeros((32, 1024), np.float32)
        for b in range(16):
            job, jb = b // 8, b % 8
            seq = x_prompt[p] if job == 0 else x_sample[c]
            start = (qr * 4096 if job == 0 else 0) + jb * 512
            if start - 1 >= 0:
                halo[2 * b] = seq[start - 1]
            if start + 512 < seq.shape[0]:
                halo[2 * b + 1] = seq[start + 512]
        pos = np.concatenate([np.arange(qr * 4096, (qr + 1) * 4096), np.arange(4096)])
        m = dict(shared)
        m.update({"xTq": f(own.T), "xTh": f(halo.T), "xTkv": xTkv[p], "xtok": f(own), "rope_q": _rope_table(pos)})
        maps.append(m)
    return maps


def kernel(**inputs):
    maps = make_in_maps(**inputs)
    nc = build_program()
    res = run_bass_kernel_spmd(nc, maps, core_ids=list(range(NCORES)))
    y_prompt = np.zeros((2, 16384, 1024), np.float32)
    y_sample = np.zeros((8, 4096, 1024), np.float32)
    for c in range(NCORES):
        y = np.asarray(res.results[c]["y"], dtype=np.float32)
        p, qr = c // 4, c % 4
        y_prompt[p, qr * 4096:(qr + 1) * 4096] = y[0:4096]
        y_sample[c] = y[4096:8192]
    return (y_prompt, y_sample)
```
